# Optimizing a Trainium2 kernel written in Bass

```python
import math, functools
import jax, jax.numpy as jnp
from jax import lax
import numpy as np

D_MODEL = 2048
BATCH = 16
SEQ = 2048
DEPTH = 2

GRID_W = 64
CTX_LEN = 256
Q_BLOCK = 128
ROPE_THETA = 10000.0
NORM_EPS = 1e-6
D_FF = 5632
N_SUB = 3
HALF_STEP = 0.5
N_EVEN = (DEPTH + 1) // 2
N_ODD = DEPTH // 2
DEEPNORM_ALPHA = (2 * DEPTH) ** 0.25
DEEPNORM_BETA = (8 * DEPTH) ** -0.25

MLA_HEADS = 8
MLA_Q_RANK = 512
MLA_KV_RANK = 256
MLA_NOPE = 128
MLA_ROPE = 64
MLA_V = 128
MLA_SCALE = (MLA_NOPE + MLA_ROPE) ** -0.5
GQA_HEADS = 8
GQA_KV_HEADS = 2
GQA_HEAD_DIM = 128
GQA_SCALE = GQA_HEAD_DIM ** -0.5
DIFF_HEADS = 8
DIFF_HEAD_DIM = 128
DIFF_SCALE = DIFF_HEAD_DIM ** -0.5

EVEN_SPLITS = (MLA_Q_RANK, MLA_KV_RANK, MLA_ROPE, GQA_HEADS * GQA_HEAD_DIM,
               GQA_KV_HEADS * GQA_HEAD_DIM, GQA_KV_HEADS * GQA_HEAD_DIM)
EVEN_IN = MLA_Q_RANK + MLA_KV_RANK + MLA_ROPE + (GQA_HEADS + 2 * GQA_KV_HEADS) * GQA_HEAD_DIM
EVEN_OUT = MLA_HEADS * MLA_V + GQA_HEADS * GQA_HEAD_DIM
DIFF_IN = 3 * DIFF_HEADS * 2 * DIFF_HEAD_DIM
DIFF_OUT = DIFF_HEADS * 2 * DIFF_HEAD_DIM

kernel_name = "hybrid_mla_gqa_diffattn_macaron_dit"


def layer_norm(x, g, b):
    xf = x.astype(jnp.float32)
    mu = jnp.mean(xf, -1, keepdims=True)
    var = jnp.mean(jnp.square(xf - mu), -1, keepdims=True)
    return ((xf - mu) * lax.rsqrt(var + NORM_EPS) * g + b).astype(x.dtype)


def rms_norm(x, g):
    xf = x.astype(jnp.float32)
    return (xf * lax.rsqrt(jnp.mean(xf * xf, -1, keepdims=True) + NORM_EPS) * g).astype(x.dtype)


def swiglu(h, w1, w3, w2):
    return (jax.nn.silu(h @ w1) * (h @ w3)) @ w2


def modulate(h, mod, j):
    return h * (1 + mod[:, j, 1]) + mod[:, j, 0]


def post_norm_residual(xs, mod, j, y, g, b):
    return layer_norm(DEEPNORM_ALPHA * xs + mod[:, j, 2] * y, g, b)


def ffn_half_step(xs, mod, j, w1, w3, w2, g, b):
    y = HALF_STEP * swiglu(modulate(xs, mod, j), w1, w3, w2)
    return post_norm_residual(xs, mod, j, y, g, b)


def axial_rope(rows, rot_dim):
    r, col = jnp.meshgrid(jnp.arange(rows, dtype=jnp.float32),
                          jnp.arange(GRID_W, dtype=jnp.float32), indexing="ij")
    n_freq = rot_dim // 4
    inv_freq = ROPE_THETA ** (-jnp.arange(n_freq, dtype=jnp.float32) / n_freq)
    ang = jnp.concatenate([r.reshape(-1, 1) * inv_freq, col.reshape(-1, 1) * inv_freq], -1)
    return jnp.cos(ang), jnp.sin(ang)


def apply_rope(x, cos, sin):
    half = x.shape[-1] // 2
    xf = x.astype(jnp.float32)
    x1, x2 = xf[..., :half], xf[..., half:]
    cs, sn = cos[None, :, None, :], sin[None, :, None, :]
    return jnp.concatenate([x1 * cs - x2 * sn, x2 * cs + x1 * sn], -1).astype(x.dtype)


def flatten_heads(y):
    return y.reshape(y.shape[0], y.shape[1], -1)


def sweep_query_blocks(fn, q):
    b, n = q.shape[:2]
    nb = n // Q_BLOCK
    blocks = jnp.moveaxis(q.reshape(b, nb, Q_BLOCK, *q.shape[2:]), 1, 0)
    out = lax.map(fn, blocks)
    return jnp.moveaxis(out, 0, 1).reshape(b, n, *out.shape[3:])


def grouped_softmax_attention(q, k, v, scale):
    s = jnp.einsum("bqhgd,bkhd->bhgqk", q, k, preferred_element_type=jnp.float32) * scale
    p = jax.nn.softmax(s, axis=-1).astype(v.dtype)
    return jnp.einsum("bhgqk,bkhd->bqhgd", p, v)


def diff_softmax_attention(q, k, v, lam):
    s = jnp.einsum("bqhjd,bkhjd->bhjqk", q, k, preferred_element_type=jnp.float32) * DIFF_SCALE
    p = jax.nn.softmax(s, axis=-1)
    a = (p[:, :, 0] - lam * p[:, :, 1]).astype(v.dtype)
    return jnp.einsum("bhqk,bkhd->bqhd", a, v)


def two_stream_attention(attend, q, k, v, q_c, k_c, v_c, need_ctx):
    k_all = jnp.concatenate([k_c, k], axis=1)
    v_all = jnp.concatenate([v_c, v], axis=1)
    y = sweep_query_blocks(lambda qb: attend(qb, k_all, v_all), q)
    y_c = attend(q_c, k_c, v_c) if need_ctx else None
    return y, y_c


def split_even(z):
    cuts = np.cumsum(EVEN_SPLITS)[:-1].tolist()
    return jnp.split(z, cuts, axis=-1)


def mla_qkv(cq, ckv, kr, g_cq, g_ckv, w_uq, w_ukv, rope):
    b, n = cq.shape[:2]
    q = (rms_norm(cq, g_cq) @ w_uq).reshape(b, n, MLA_HEADS, MLA_NOPE + MLA_ROPE)
    kv = (rms_norm(ckv, g_ckv) @ w_ukv).reshape(b, n, MLA_HEADS, MLA_NOPE + MLA_V)
    q_nope, q_rot = q[..., :MLA_NOPE], q[..., MLA_NOPE:]
    k_nope, v = kv[..., :MLA_NOPE], kv[..., MLA_NOPE:]
    k_rot = kr[:, :, None, :]
    if rope is not None:
        q_rot = apply_rope(q_rot, *rope)
        k_rot = apply_rope(k_rot, *rope)
    k = jnp.concatenate([k_nope, jnp.broadcast_to(k_rot, (b, n, MLA_HEADS, MLA_ROPE))], -1)
    q = jnp.concatenate([q_nope, q_rot], -1)
    return q[:, :, :, None, :], k, v


def gqa_qkv(q, k, v, g_q, g_k, rope):
    b, n = q.shape[:2]
    q = rms_norm(q.reshape(b, n, GQA_HEADS, GQA_HEAD_DIM), g_q)
    k = rms_norm(k.reshape(b, n, GQA_KV_HEADS, GQA_HEAD_DIM), g_k)
    v = v.reshape(b, n, GQA_KV_HEADS, GQA_HEAD_DIM)
    if rope is not None:
        q = apply_rope(q, *rope)
        k = apply_rope(k, *rope)
    q = q.reshape(b, n, GQA_KV_HEADS, GQA_HEADS // GQA_KV_HEADS, GQA_HEAD_DIM)
    return q, k, v


def mla_gqa_mixer(h_lat, h_ctx, rope_mla, rope_gqa, w_in, g_cq, g_ckv, w_uq, w_ukv,
                  g_q, g_k, w_o, need_ctx):
    def project(h, rope_a, rope_b):
        cq, ckv, kr, qb, kb, vb = split_even(h @ w_in)
        return (mla_qkv(cq, ckv, kr, g_cq, g_ckv, w_uq, w_ukv, rope_a),
                gqa_qkv(qb, kb, vb, g_q, g_k, rope_b))

    (qa, ka, va), (qb, kb, vb) = project(h_lat, rope_mla, rope_gqa)
    (qa_c, ka_c, va_c), (qb_c, kb_c, vb_c) = project(h_ctx, None, None)
    att_a = functools.partial(grouped_softmax_attention, scale=MLA_SCALE)
    att_b = functools.partial(grouped_softmax_attention, scale=GQA_SCALE)
    ya, ya_c = two_stream_attention(att_a, qa, ka, va, qa_c, ka_c, va_c, need_ctx)
    yb, yb_c = two_stream_attention(att_b, qb, kb, vb, qb_c, kb_c, vb_c, need_ctx)

    def merge(a, bb):
        return jnp.concatenate([flatten_heads(a), flatten_heads(bb)], -1) @ w_o

    return merge(ya, yb), (merge(ya_c, yb_c) if need_ctx else None)


def diff_mixer(h_lat, h_ctx, rope, w_in, lq1, lk1, lq2, lk2, g_sub, w_o, lambda_init, need_ctx):
    lam = (jnp.exp(jnp.sum(lq1.astype(jnp.float32) * lk1.astype(jnp.float32)))
           - jnp.exp(jnp.sum(lq2.astype(jnp.float32) * lk2.astype(jnp.float32))) + lambda_init)

    def project(h, rp):
        b, n = h.shape[:2]
        q, k, v = jnp.split(h @ w_in, 3, axis=-1)
        q = q.reshape(b, n, 2 * DIFF_HEADS, DIFF_HEAD_DIM)
        k = k.reshape(b, n, 2 * DIFF_HEADS, DIFF_HEAD_DIM)
        if rp is not None:
            q = apply_rope(q, *rp)
            k = apply_rope(k, *rp)
        q = q.reshape(b, n, DIFF_HEADS, 2, DIFF_HEAD_DIM)
        k = k.reshape(b, n, DIFF_HEADS, 2, DIFF_HEAD_DIM)
        v = v.reshape(b, n, DIFF_HEADS, 2 * DIFF_HEAD_DIM)
        return q, k, v

    q, k, v = project(h_lat, rope)
    q_c, k_c, v_c = project(h_ctx, None)
    attend = lambda qq, kk, vv: diff_softmax_attention(qq, kk, vv, lam)
    y, y_c = two_stream_attention(attend, q, k, v, q_c, k_c, v_c, need_ctx)

    def finish(yy):
        return flatten_heads(rms_norm(yy, g_sub) * (1.0 - lambda_init)) @ w_o

    return finish(y), (finish(y_c) if need_ctx else None)


def setup_inputs(seed: int = 0) -> dict:
    key = jax.random.key(seed)
    ks = iter(jax.random.split(key, 32))
    nrm = lambda shape, scale: jax.random.normal(next(ks), shape, jnp.float32) * scale
    gain = lambda shape: 1.0 + nrm(shape, 0.02)
    D, F = D_MODEL, D_FF
    return {
        "x": nrm((BATCH, SEQ, D), 1.0),
        "c": nrm((BATCH, D), 1.0),
        "ctx": nrm((BATCH, CTX_LEN, D), 1.0),
        "c_ctx": nrm((D,), 1.0),
        "w_ada": nrm((DEPTH, D, N_SUB * 3 * D), D ** -0.5),
        "b_ada": nrm((DEPTH, N_SUB * 3 * D), 0.02),
        "ln_g": gain((DEPTH, N_SUB, D)),
        "ln_b": nrm((DEPTH, N_SUB, D), 0.02),
        "ffn_w1": nrm((DEPTH, 2, D, F), D ** -0.5),
        "ffn_w3": nrm((DEPTH, 2, D, F), D ** -0.5),
        "ffn_w2": nrm((DEPTH, 2, F, D), DEEPNORM_BETA * F ** -0.5),
        "mg_w_in": nrm((N_EVEN, D, EVEN_IN), D ** -0.5),
        "mla_g_cq": gain((N_EVEN, MLA_Q_RANK)),
        "mla_g_ckv": gain((N_EVEN, MLA_KV_RANK)),
        "mla_w_uq": nrm((N_EVEN, MLA_Q_RANK, MLA_HEADS * (MLA_NOPE + MLA_ROPE)), MLA_Q_RANK ** -0.5),
        "mla_w_ukv": nrm((N_EVEN, MLA_KV_RANK, MLA_HEADS * (MLA_NOPE + MLA_V)), MLA_KV_RANK ** -0.5),
        "gqa_g_q": gain((N_EVEN, GQA_HEAD_DIM)),
        "gqa_g_k": gain((N_EVEN, GQA_HEAD_DIM)),
        "mg_w_o": nrm((N_EVEN, EVEN_OUT, D), DEEPNORM_BETA * EVEN_OUT ** -0.5),
        "diff_w_in": nrm((N_ODD, D, DIFF_IN), D ** -0.5),
        "diff_lq1": nrm((N_ODD, DIFF_HEAD_DIM), 0.1),
        "diff_lk1": nrm((N_ODD, DIFF_HEAD_DIM), 0.1),
        "diff_lq2": nrm((N_ODD, DIFF_HEAD_DIM), 0.1),
        "diff_lk2": nrm((N_ODD, DIFF_HEAD_DIM), 0.1),
        "diff_g_sub": gain((N_ODD, 2 * DIFF_HEAD_DIM)),
        "diff_w_o": nrm((N_ODD, DIFF_OUT, D), DEEPNORM_BETA * DIFF_OUT ** -0.5),
    }


def reference(x, c, ctx, c_ctx, w_ada, b_ada, ln_g, ln_b, ffn_w1, ffn_w3, ffn_w2,
              mg_w_in, mla_g_cq, mla_g_ckv, mla_w_uq, mla_w_ukv, gqa_g_q, gqa_g_k, mg_w_o,
              diff_w_in, diff_lq1, diff_lk1, diff_lq2, diff_lk2, diff_g_sub, diff_w_o):
    b, n, d = x.shape
    rows = n // GRID_W
    rope_mla = axial_rope(rows, MLA_ROPE)
    rope_gqa = axial_rope(rows, GQA_HEAD_DIM)
    rope_diff = axial_rope(rows, DIFF_HEAD_DIM)
    s_lat = jax.nn.silu(c)
    s_ctx = jax.nn.silu(c_ctx)[None]
    x_lat, x_ctx = x, ctx
    for i in range(DEPTH):
        need_ctx = i < DEPTH - 1
        mod_lat = (s_lat @ w_ada[i] + b_ada[i]).reshape(b, N_SUB, 3, 1, d)
        mod_ctx = (s_ctx @ w_ada[i] + b_ada[i]).reshape(1, N_SUB, 3, 1, d)

        x_lat = ffn_half_step(x_lat, mod_lat, 0, ffn_w1[i, 0], ffn_w3[i, 0], ffn_w2[i, 0], ln_g[i, 0], ln_b[i, 0])
        x_ctx = ffn_half_step(x_ctx, mod_ctx, 0, ffn_w1[i, 0], ffn_w3[i, 0], ffn_w2[i, 0], ln_g[i, 0], ln_b[i, 0])

        h_lat = modulate(x_lat, mod_lat, 1)
        h_ctx = modulate(x_ctx, mod_ctx, 1)
        if i % 2 == 0:
            e = i // 2
            y_lat, y_ctx = mla_gqa_mixer(h_lat, h_ctx, rope_mla, rope_gqa, mg_w_in[e], mla_g_cq[e],
                                         mla_g_ckv[e], mla_w_uq[e], mla_w_ukv[e], gqa_g_q[e],
                                         gqa_g_k[e], mg_w_o[e], need_ctx)
        else:
            o = i // 2
            lambda_init = 0.8 - 0.6 * math.exp(-0.3 * i)
            y_lat, y_ctx = diff_mixer(h_lat, h_ctx, rope_diff, diff_w_in[o], diff_lq1[o], diff_lk1[o],
                                      diff_lq2[o], diff_lk2[o], diff_g_sub[o], diff_w_o[o],
                                      lambda_init, need_ctx)
        x_lat = post_norm_residual(x_lat, mod_lat, 1, y_lat, ln_g[i, 1], ln_b[i, 1])

        x_lat = ffn_half_step(x_lat, mod_lat, 2, ffn_w1[i, 1], ffn_w3[i, 1], ffn_w2[i, 1], ln_g[i, 2], ln_b[i, 2])
        if need_ctx:
            x_ctx = post_norm_residual(x_ctx, mod_ctx, 1, y_ctx, ln_g[i, 1], ln_b[i, 1])
            x_ctx = ffn_half_step(x_ctx, mod_ctx, 2, ffn_w1[i, 1], ffn_w3[i, 1], ffn_w2[i, 1], ln_g[i, 2], ln_b[i, 2])
    return x_lat
```

```python
import math
from contextlib import ExitStack, contextmanager
import numpy as np
import concourse.bass as bass
import concourse.mybir as mybir
from concourse.bass_utils import run_bass_kernel_spmd

F32 = mybir.dt.float32
BF16 = mybir.dt.bfloat16
ALU = mybir.AluOpType
AF = mybir.ActivationFunctionType
AX = mybir.AxisListType

NCORES = 8
D = 2048
KC = 16
FF = 5632
FC = 44
TB = 512
NBLK = 9
SEQ = 2048
CTX = 256
ALPHA = float((2 * 2) ** 0.25)
EPS = 1e-6
NMOD = 9 * KC
LAMBDA_INIT1 = float(0.8 - 0.6 * math.exp(-0.3 * 1))


class KB:
    def __init__(self, nc):
        self.nc = nc
        self.es = ExitStack()
        self.E = {"pe": nc.tensor, "act": nc.scalar, "dve": nc.vector, "pool": nc.gpsimd, "sp": nc.sync}
        self.sem, self.cnt = {}, {}
        self.waited = {e: {} for e in self.E}
        self.reg = {}
        self.uid = 0
        for e in ("pe", "act", "dve", "pool"):
            self.new_sem(e)
        self.new_sem("misc")
        self.new_sem("conv")
        self.shared = {"misc", "conv", "ld_kv"}

    def new_sem(self, key):
        self.sem[key] = self.es.enter_context(self.nc.semaphore("s_%s" % str(key).replace(" ", "")))
        self.cnt[key] = 0

    def _wait(self, eng, need):
        for k, v in need.items():
            if eng == "pe" and k == "pe":
                continue
            if k in self.shared:
                v = self.cnt[k]
            if self.waited[eng].get(k, 0) < v:
                self.E[eng].wait_ge(self.sem[k], v)
                self.waited[eng][k] = v

    def _deps(self, r, w):
        need = {}
        for key in r:
            ent = self.reg.get(key)
            if ent and ent[0]:
                k, v = ent[0]
                need[k] = max(need.get(k, 0), v)
        for key in w:
            ent = self.reg.get(key)
            if ent:
                if ent[0]:
                    k, v = ent[0]
                    need[k] = max(need.get(k, 0), v)
                for k, v in ent[1].items():
                    need[k] = max(need.get(k, 0), v)
        return need

    def _commit(self, tok, r, w):
        k, v = tok
        for key in r:
            ent = self.reg.setdefault(key, [None, {}])
            ent[1][k] = max(ent[1].get(k, 0), v)
        for key in w:
            self.reg[key] = [tok, {}]

    def op(self, eng, fn, r=(), w=(), inc=True):
        self._wait(eng, self._deps(r, w))
        ins = fn(self.E[eng])
        if inc:
            self.cnt[eng] += 1
            ins.then_inc(self.sem[eng], 1)
            tok = (eng, self.cnt[eng])
        else:
            tok = (eng, self.cnt[eng] + 1)
        self._commit(tok, r, w)
        return ins

    def dma(self, q, out, in_, r=(), w=(), sem="misc"):
        if sem not in self.sem:
            self.new_sem(sem)
        self._wait(q, self._deps(r, w))
        ins = self.E[q].dma_start(out=out, in_=in_)
        self.cnt[sem] += 16
        ins.then_inc(self.sem[sem], 16)
        self._commit((sem, self.cnt[sem]), r, w)

    def wait_all(self, e, k):
        v = self.cnt[k]
        if v > 0 and self.waited[e].get(k, 0) < v:
            self.E[e].wait_ge(self.sem[k], v)
            self.waited[e][k] = v

    def barrier(self, skip=("conv",)):
        for e in self.E:
            for k in self.sem:
                if k in skip:
                    continue
                v = self.cnt[k]
                if v > 0 and self.waited[e].get(k, 0) < v:
                    self.E[e].wait_ge(self.sem[k], v)
                    self.waited[e][k] = v
        self.reg = {}

    @contextmanager
    def scope(self):
        self.barrier()
        st = ExitStack()
        with st:
            yield st
            self.barrier()

    def name(self, base):
        self.uid += 1
        return "%s_%d" % (base, self.uid)

    def sb(self, st, base, shape, dt):
        return st.enter_context(self.nc.sbuf_tensor(self.name(base), list(shape), dt))

    def ps(self, st, base, shape, dt):
        return st.enter_context(self.nc.psum_tensor(self.name(base), list(shape), dt))


class Ring:
    def __init__(self, kb, name, slots, q="pool"):
        self.kb, self.name, self.slots, self.q = kb, name, slots, q
        self.units = []
        self.pre = {}
        self.issued = 0
        for s in range(len(slots)):
            if (name, s) not in kb.sem:
                kb.new_sem((name, s))

    def issue_upto(self, n):
        n = min(n, len(self.units))
        while self.issued < n:
            i = self.issued
            s = i % len(self.slots)
            if i in self.pre:
                self.pre[i]()
            un = self.units[i]
            dst = self.slots[s][:]
            if tuple(un.shape) != tuple(dst.shape):
                dst = self.slots[s][:, :, 0:un.shape[-1]]
            self.kb.dma(self.q, dst, un, w=[(self.name, s)], sem=(self.name, s))
            self.issued += 1

    def key(self, i):
        return (self.name, i % len(self.slots))

    def buf(self, i):
        return self.slots[i % len(self.slots)]


def build_program(dbg=False, stages=None):
    nc = bass.Bass("TRN2", target_bir_lowering=False)
    dt = lambda name, shape, d=F32: nc.dram_tensor(name, list(shape), d, kind="ExternalInput").ap()
    x_in = dt("x", [2, SEQ, D])
    ctx_in = dt("ctx", [2, CTX, D])
    cT_in = dt("cT", [128, KC, 3])
    w_ada = dt("w_ada", [2, D, 9 * D])
    b_adaT = dt("b_adaT", [2, 128, NMOD * 3])
    ln_gT = dt("ln_gT", [128, 6, KC])
    ln_bT = dt("ln_bT", [128, 6, KC])
    ffn_w1 = dt("ffn_w1", [2, 2, D, FF])
    ffn_w3 = dt("ffn_w3", [2, 2, D, FF])
    ffn_w2 = dt("ffn_w2", [2, 2, FF, D])
    mg_w_in = dt("mg_w_in", [D, 2368])
    mla_w_uq = dt("mla_w_uq", [512, 1536])
    mla_w_ukv = dt("mla_w_ukv", [256, 2048])
    mg_w_o = dt("mg_w_o", [D, D])
    diff_w_in = dt("diff_w_in", [D, 6144])
    diff_w_o = dt("diff_w_o", [D, D])
    g0_in = dt("g0", [128, 1024])
    ropeM_in = dt("ropeM", [SEQ, 2, 256])
    ropeG_in = dt("ropeG", [SEQ, 2, 1024])
    lqk_in = dt("lqk", [128, 4, 128])
    gsub_in = dt("gsub", [128, 256])
    out = nc.dram_tensor("out", [2, SEQ, D], F32, kind="ExternalOutput").ap()
    QT_all = nc.dram_tensor("QT_all", [2, 18, 128, 24, 128], BF16, kind="Internal").ap()
    KT_all = nc.dram_tensor("KT_all", [2, 128, 17, 18 * 128], BF16, kind="Internal").ap()
    V_all = nc.dram_tensor("V_all", [2, 18, 128, 2048], BF16, kind="Internal").ap()
    OT_all = nc.dram_tensor("OT_all", [NBLK, 128, KC, TB], BF16, kind="Internal").ap()
    W1c = nc.dram_tensor("W1c", [4, 22, 128, KC, 256], BF16, kind="Internal").ap()
    W3c = nc.dram_tensor("W3c", [4, 22, 128, KC, 256], BF16, kind="Internal").ap()
    W2c = nc.dram_tensor("W2c", [4, KC, 128, FC, 128], BF16, kind="Internal").ap()
    XT = nc.dram_tensor("XT", [NBLK, 128, KC, TB], F32, kind="Internal").ap()
    if dbg:
        dbg_xt = nc.dram_tensor("dbg_xt", [NBLK, 128, KC, TB], F32, kind="ExternalOutput").ap()

    kb = KB(nc)
    with kb.es:
        top = kb.es
        identF = kb.sb(top, "identF", [128, 128], F32)
        onesF = kb.sb(top, "onesF", [128, 128], F32)
        modT = [kb.sb(top, "modT%d" % l, [128, NMOD, 3], F32) for l in range(2)]
        mod1p = [kb.sb(top, "mod1p%d" % l, [128, NMOD, 3], F32) for l in range(2)]
        modh = [kb.sb(top, "modh%d" % l, [128, NMOD, 3], F32) for l in range(2)]
        lng = kb.sb(top, "lng", [128, 6, KC], F32)
        lnb = kb.sb(top, "lnb", [128, 6, KC], F32)
        sT = kb.sb(top, "sT", [128, KC, 3], BF16)
        identB = kb.sb(top, "identB", [128, 128], BF16)

        kb.op("pool", lambda e: e.memset(onesF[:], 1.0), w=["onesF"])
        kb.op("pool", lambda e: e.memset(identF[:], 0.0), w=["identF"])
        kb.op("pool", lambda e: e.affine_select(out=identF[:], in_=onesF[:], pattern=[[-1, 128]],
                                                compare_op=ALU.is_equal, fill=0.0, base=0,
                                                channel_multiplier=1), r=["onesF"], w=["identF"])
        kb.op("dve", lambda e: e.tensor_copy(out=identB[:], in_=identF[:]), r=["identF"], w=["identB"])
        kb.dma("sp", lng[:], ln_gT, w=["lng"])
        kb.dma("sp", lnb[:], ln_bT, w=["lnb"])

        def conv_jobs(f):
            l_, fi_ = f // 2, f % 2
            w1v = ffn_w1[l_, fi_].rearrange("(kc p) f -> p kc f", p=128)
            w3v = ffn_w3[l_, fi_].rearrange("(kc p) f -> p kc f", p=128)
            w2v = ffn_w2[l_, fi_].rearrange("(fc p) n -> p fc n", p=128)
            jobs = []
            for fb in range(22):
                jobs.append((W1c[f, fb], w1v[:, :, fb * 256:(fb + 1) * 256], ("W1c", f, fb)))
                jobs.append((W3c[f, fb], w3v[:, :, fb * 256:(fb + 1) * 256], ("W3c", f, fb)))
            for n in range(KC):
                jobs.append((W2c[f, n], w2v[:, :, n * 128:(n + 1) * 128], ("W2c", f, n)))
            return jobs

        conv_pending = []

        def conv_emit(n):
            for _ in range(min(n, len(conv_pending))):
                dst, src, key = conv_pending.pop(0)
                kb.dma("pool", dst, src, w=[key], sem="conv")

        def blk_rows(blk):
            return 2 if blk == 0 else (0 if blk <= 4 else 1)

        def tile_src(blk, i):
            if blk == 0:
                return ctx_in[i // 2, (i % 2) * 128:(i % 2) * 128 + 128, :]
            b = 0 if blk <= 4 else 1
            t0 = ((blk - 1) % 4) * TB + i * 128
            return x_in[b, t0:t0 + 128, :]

        def stage_in():
            with kb.scope() as st:
                xtok = [kb.sb(st, "xtok", [128, D], F32) for _ in range(2)]
                xbs = [kb.sb(st, "xbs", [128, KC, TB], F32) for _ in range(2)]
                pst = [kb.ps(st, "pst", [128, 512], F32) for _ in range(4)]
                n = 0
                for blk in range(NBLK):
                    xb = xbs[blk % 2]
                    for i in range(4):
                        t = xtok[n % 2]
                        kb.dma("sp", t[:], tile_src(blk, i), w=[("xtok", n % 2)], sem=("ld_xtok", n % 2))
                        for q in range(4):
                            pb = pst[(n * 4 + q) % 4]
                            for kk in range(4):
                                kc = q * 4 + kk
                                kb.op("pe", lambda e, pb=pb, kk=kk, kc=kc, t=t: e.transpose(
                                    out=pb[:, kk * 128:(kk + 1) * 128], in_=t[:, kc * 128:(kc + 1) * 128],
                                    identity=identF[:]), r=[("xtok", n % 2), "identF"],
                                    w=[("pst", (n * 4 + q) % 4)], inc=(kk == 3))
                            eng = "dve" if q % 2 == 0 else "act"
                            dst = xb[:, q * 4:(q + 1) * 4, i * 128:(i + 1) * 128]
                            src = pb[:].rearrange("p (a b) -> p a b", a=4)
                            if eng == "dve":
                                kb.op("dve", lambda e, dst=dst, src=src: e.tensor_copy(out=dst, in_=src),
                                      r=[("pst", (n * 4 + q) % 4)], w=[("xb", blk % 2, i, q)])
                            else:
                                kb.op("act", lambda e, dst=dst, src=src: e.activation(out=dst, in_=src, func=AF.Copy),
                                      r=[("pst", (n * 4 + q) % 4)], w=[("xb", blk % 2, i, q)])
                        n += 1
                    kb.dma("sp", XT[blk], xb[:], r=[("xb", blk % 2, i, q) for i in range(4) for q in range(4)],
                           w=[("XT", blk)], sem=("st_xb", blk % 2))

        def stage_out():
            with kb.scope() as st:
                xtok = [kb.sb(st, "xtok", [128, D], F32) for _ in range(2)]
                xbs = [kb.sb(st, "xbs", [128, KC, TB], F32) for _ in range(2)]
                pst = [kb.ps(st, "pst", [128, 512], F32) for _ in range(4)]
                n = 0
                for blk in range(1, NBLK):
                    xb = xbs[blk % 2]
                    kb.dma("sp", xb[:], XT[blk], r=[("XT", blk)], w=[("xb", blk % 2)], sem=("ld_xb", blk % 2))
                    b = 0 if blk <= 4 else 1
                    for i in range(4):
                        t = xtok[n % 2]
                        for q in range(4):
                            pb = pst[(n * 4 + q) % 4]
                            for kk in range(4):
                                kc = q * 4 + kk
                                kb.op("pe", lambda e, pb=pb, kk=kk, kc=kc, xb=xb, i=i: e.transpose(
                                    out=pb[:, kk * 128:(kk + 1) * 128], in_=xb[:, kc, i * 128:(i + 1) * 128],
                                    identity=identF[:]), r=[("xb", blk % 2), "identF"],
                                    w=[("pst", (n * 4 + q) % 4)], inc=(kk == 3))
                            dst = t[:, q * 512:(q + 1) * 512]
                            if q % 2 == 0:
                                kb.op("dve", lambda e, dst=dst, pb=pb: e.tensor_copy(out=dst, in_=pb[:]),
                                      r=[("pst", (n * 4 + q) % 4)], w=[("xtok", n % 2, q)])
                            else:
                                kb.op("act", lambda e, dst=dst, pb=pb: e.activation(out=dst, in_=pb[:], func=AF.Copy),
                                      r=[("pst", (n * 4 + q) % 4)], w=[("xtok", n % 2, q)])
                        t0 = ((blk - 1) % 4) * TB + i * 128
                        kb.dma("sp", out[b, t0:t0 + 128, :], t[:], r=[("xtok", n % 2, q) for q in range(4)],
                               w=[("out", blk, i)], sem=("st_xtok", n % 2))
                        n += 1

        def stage_silu():
            with kb.scope() as st:
                cf = kb.sb(st, "cf", [128, KC, 3], F32)
                kb.dma("sp", cf[:], cT_in, w=["cf"])
                kb.op("act", lambda e: e.activation(out=sT[:], in_=cf[:], func=AF.Silu), r=["cf"], w=["sT"])

        def stage_mod(l):
            with kb.scope() as st:
                slots = [kb.sb(st, "wa", [128, KC, 512], BF16) for _ in range(3)]
                ring = Ring(kb, "rw", slots)
                wv = w_ada[l].rearrange("(kc p) n -> p kc n", p=128)
                ring.units = [wv[:, :, c * 512:(c + 1) * 512] for c in range(36)]
                pm = kb.ps(st, "pm", [128, 512], F32)
                bt = kb.sb(st, "bt", [128, NMOD * 3], F32)
                kb.dma("sp", bt[:], b_adaT[l], w=["bt"])
                ring.issue_upto(3)
                for c in range(36):
                    wa = ring.buf(c)
                    for sub in range(4):
                        ch = c * 4 + sub
                        for kc in range(KC):
                            kb.op("pe", lambda e, wa=wa, sub=sub, kc=kc, ch=ch: e.matmul(
                                pm[:, ch * 3:ch * 3 + 3], lhsT=wa[:, kc, sub * 128:(sub + 1) * 128],
                                rhs=sT[:, kc, :], start=(kc == 0), stop=(kc == KC - 1)),
                                r=[ring.key(c), "sT"], w=["pm"], inc=(kc == KC - 1 and sub == 3))
                    ring.issue_upto(c + 4)
                m2 = modT[l][:].rearrange("p a b -> p (a b)")
                kb.op("dve", lambda e: e.tensor_tensor(out=m2, in0=pm[:, 0:NMOD * 3], in1=bt[:], op=ALU.add),
                      r=["pm", "bt"], w=[("modT", l)])
                kb.op("dve", lambda e: e.tensor_scalar_add(out=mod1p[l][:].rearrange("p a b -> p (a b)"), in0=m2,
                                                           scalar1=1.0), r=[("modT", l)], w=[("mod1p", l)])
                kb.op("dve", lambda e: e.tensor_scalar_mul(out=modh[l][:].rearrange("p a b -> p (a b)"), in0=m2,
                                                           scalar1=0.5), r=[("modT", l)], w=[("modh", l)])

        def mod_thunks(l, st, pm):
            slots = [kb.sb(st, "wa", [128, KC, 512], BF16) for _ in range(3)]
            ring = Ring(kb, "rw", slots)
            wv = w_ada[l].rearrange("(kc p) n -> p kc n", p=128)
            ring.units = [wv[:, :, c * 512:(c + 1) * 512] for c in range(36)]
            bt = kb.sb(st, "bt", [128, NMOD * 3], F32)
            th = []

            def first():
                kb.dma("sp", bt[:], b_adaT[l], w=["bt"])
                ring.issue_upto(3)
            th.append(first)

            def chunk(c):
                wa = ring.buf(c)
                for sub in range(4):
                    ch = c * 4 + sub
                    for kc in range(KC):
                        kb.op("pe", lambda e, sub=sub, kc=kc, ch=ch: e.matmul(
                            pm[:, ch * 3:ch * 3 + 3], lhsT=wa[:, kc, sub * 128:(sub + 1) * 128],
                            rhs=sT[:, kc, :], start=(kc == 0), stop=(kc == KC - 1)),
                            r=[ring.key(c), "sT"], w=["pm"], inc=(kc == KC - 1 and sub == 3))
                ring.issue_upto(c + 4)
            for c in range(36):
                th.append(lambda c=c: chunk(c))

            def last():
                m2 = modT[l][:].rearrange("p a b -> p (a b)")
                kb.op("dve", lambda e: e.tensor_tensor(out=m2, in0=pm[:, 0:NMOD * 3], in1=bt[:], op=ALU.add),
                      r=["pm", "bt"], w=[("modT", l)])
                kb.op("dve", lambda e: e.tensor_scalar_add(out=mod1p[l][:].rearrange("p a b -> p (a b)"), in0=m2,
                                                           scalar1=1.0), r=[("modT", l)], w=[("mod1p", l)])
                kb.op("dve", lambda e: e.tensor_scalar_mul(out=modh[l][:].rearrange("p a b -> p (a b)"), in0=m2,
                                                           scalar1=0.5), r=[("modT", l)], w=[("modh", l)])
            th.append(last)
            return th

        def mod_ap(t, l, j, s, kc, row):
            idx = (j * 3 + s) * KC + kc
            return t[l][:, idx, row:row + 1]

        def emit_modulate(l, j, blk, xs, hT, hkey):
            row = blk_rows(blk)
            for half in range(2):
                kb.dma("sp", xs[:], XT[blk][:, half * 8:(half + 1) * 8, :], r=[("XT", blk)], w=["xs"], sem="ld_xs")
                for k8 in range(8):
                    kc = half * 8 + k8
                    kb.op("dve", lambda e, k8=k8, kc=kc: e.tensor_scalar(
                        out=hT[:, kc, :], in0=xs[:, k8, :], scalar1=mod_ap(mod1p, l, j, 1, kc, row),
                        scalar2=mod_ap(modT, l, j, 0, kc, row), op0=ALU.mult, op1=ALU.add),
                        r=["xs", ("mod1p", l), ("modT", l)], w=[(hkey, kc)])

        def emit_outproj_epilogue(l, j, blk, nfc, uT, ukey, w2ring, w2base, xz, gate_t, lnidx, py, pst,
                                  tmp, after_load=None, xk="xz", defer=False, prev_tail=None):
            row = blk_rows(blk)
            yg, sq, s1, s2, mean, msq, var, rstd = tmp
            kb.dma("sp", xz[:], XT[blk], r=[("XT", blk)], w=[(xk, kc) for kc in range(KC)], sem=("ld_xz", xk))
            if after_load is not None:
                after_load()
            for ncn in range(KC):
                u = w2base + ncn
                wb = w2ring.buf(u)
                pb = py[ncn % 2]
                for fc in range(nfc):
                    kb.op("pe", lambda e, wb=wb, fc=fc, pb=pb: e.matmul(
                        pb[:], lhsT=wb[:, fc, :], rhs=uT[:, fc, :], start=(fc == 0), stop=(fc == nfc - 1)),
                        r=[w2ring.key(u), (ukey, fc)], w=[("py", ncn % 2)], inc=(fc == nfc - 1))
                w2ring.issue_upto(u + 1 + len(w2ring.slots))
                if prev_tail:
                    for _ in range(5):
                        if prev_tail:
                            prev_tail.pop(0)()
                ygb = yg[ncn % 2]
                kb.op("act", lambda e, ygb=ygb, pb=pb, ncn=ncn: e.activation(
                    out=ygb[:], in_=pb[:], func=AF.Copy, scale=mod_ap(gate_t, l, j, 2, ncn, row)),
                    r=[("py", ncn % 2), ("gate", l)], w=[("yg", ncn % 2)])
                kb.op("dve", lambda e, ygb=ygb, ncn=ncn: e.scalar_tensor_tensor(
                    out=xz[:, ncn, :], in0=xz[:, ncn, :], scalar=ALPHA, in1=ygb[:], op0=ALU.mult, op1=ALU.add),
                    r=[("yg", ncn % 2)], w=[(xk, ncn)])
                kb.op("act", lambda e, ncn=ncn: e.activation(out=sq[ncn % 2][:], in_=xz[:, ncn, :], func=AF.Square),
                      r=[(xk, ncn)], w=[("sq", ncn % 2)])
                if ncn == 0:
                    kb.op("dve", lambda e: e.tensor_copy(out=s1[:], in_=xz[:, 0, :]), r=[(xk, 0)], w=["s1"])
                    kb.op("dve", lambda e: e.tensor_copy(out=s2[:], in_=sq[0][:]), r=[("sq", 0)], w=["s2"])
                else:
                    kb.op("dve", lambda e, ncn=ncn: e.tensor_add(out=s1[:], in0=s1[:], in1=xz[:, ncn, :]),
                          r=[(xk, ncn)], w=["s1"])
                    kb.op("dve", lambda e, ncn=ncn: e.tensor_add(out=s2[:], in0=s2[:], in1=sq[ncn % 2][:]),
                          r=[("sq", ncn % 2)], w=["s2"])
            tail = []
            T = tail.append
            T(lambda: kb.op("pe", lambda e: e.matmul(pst[0][:], lhsT=onesF[:], rhs=s1[:], start=True, stop=True),
                            r=["s1", "onesF"], w=["pst0"]))
            T(lambda: kb.op("pe", lambda e: e.matmul(pst[1][:], lhsT=onesF[:], rhs=s2[:], start=True, stop=True),
                            r=["s2", "onesF"], w=["pst1"]))
            T(lambda: kb.op("dve", lambda e: e.tensor_scalar_mul(out=mean[:], in0=pst[0][:], scalar1=1.0 / D),
                            r=["pst0"], w=["mean"]))
            T(lambda: kb.op("dve", lambda e: e.tensor_tensor(out=msq[:], in0=mean[:], in1=mean[:], op=ALU.mult),
                            r=["mean"], w=["msq"]))
            T(lambda: kb.op("dve", lambda e: e.scalar_tensor_tensor(out=var[:], in0=pst[1][:], scalar=1.0 / D,
                                                                    in1=msq[:], op0=ALU.mult, op1=ALU.subtract),
                            r=["pst1", "msq"], w=["var"]))
            T(lambda: kb.op("dve", lambda e: e.tensor_scalar_add(out=var[:], in0=var[:], scalar1=EPS), r=["var"],
                            w=["var"]))
            T(lambda: kb.op("act", lambda e: e.activation(out=var[:], in_=var[:], func=AF.Sqrt), r=["var"], w=["var"]))
            T(lambda: kb.op("dve", lambda e: e.reciprocal(out=rstd[:], in_=var[:]), r=["var"], w=["rstd"]))
            for kc in range(KC):
                T(lambda kc=kc: kb.op("dve", lambda e: e.tensor_tensor(out=xz[:, kc, :], in0=xz[:, kc, :], in1=mean[:],
                                                                       op=ALU.subtract), r=["mean"], w=[(xk, kc)]))
                T(lambda kc=kc: kb.op("dve", lambda e: e.tensor_tensor(out=xz[:, kc, :], in0=xz[:, kc, :], in1=rstd[:],
                                                                       op=ALU.mult), r=["rstd"], w=[(xk, kc)]))
                T(lambda kc=kc: kb.op("act", lambda e: e.activation(
                    out=xz[:, kc, :], in_=xz[:, kc, :], func=AF.Identity, scale=lng[:, lnidx, kc:kc + 1],
                    bias=lnb[:, lnidx, kc:kc + 1]), r=["lng", "lnb"], w=[(xk, kc)]))
            T(lambda: kb.dma("sp", XT[blk], xz[:], r=[(xk, kc) for kc in range(KC)], w=[("XT", blk)], sem=("st_xz", xk)))
            if defer:
                return tail
            for t_ in tail:
                t_()
            return []

        def alloc_epi_tmp(st):
            yg = [kb.sb(st, "yg", [128, TB], F32) for _ in range(2)]
            sq = [kb.sb(st, "sq", [128, TB], F32) for _ in range(2)]
            rest = [kb.sb(st, nm, [128, TB], F32) for nm in ("s1", "s2", "mean", "msq", "var", "rstd")]
            return (yg, sq) + tuple(rest)

        def stage_ffn(l, j, fi, blocks):
            with kb.scope() as st:
                xz = kb.sb(st, "xz", [128, KC, TB], F32)
                xs = kb.sb(st, "xs", [128, 8, TB], F32)
                hT = kb.sb(st, "hT", [128, KC, TB], BF16)
                uT = kb.sb(st, "uT", [128, FC, TB], BF16)
                sil = [kb.sb(st, "sil", [128, TB], BF16) for _ in range(2)]
                tmp = alloc_epi_tmp(st)
                w1s = [kb.sb(st, "w1s", [128, KC, 256], BF16) for _ in range(2)]
                w3s = [kb.sb(st, "w3s", [128, KC, 256], BF16) for _ in range(2)]
                w2s = [kb.sb(st, "w2s", [128, FC, 128], BF16) for _ in range(3)]
                pu1 = [kb.ps(st, "pu1", [128, TB], F32) for _ in range(2)]
                pu3 = [kb.ps(st, "pu3", [128, TB], F32) for _ in range(2)]
                py = [kb.ps(st, "py", [128, TB], F32) for _ in range(2)]
                pst = [kb.ps(st, "pst", [128, TB], F32) for _ in range(2)]
                r1 = Ring(kb, "r1", w1s)
                r3 = Ring(kb, "r3", w3s)
                r2 = Ring(kb, "r2", w2s)
                f = l * 2 + fi
                ng = len(blocks)
                for g in range(ng):
                    if f == 0 and g == 0:
                        w1v = ffn_w1[l, fi].rearrange("(kc p) f -> p kc f", p=128)
                        w3v = ffn_w3[l, fi].rearrange("(kc p) f -> p kc f", p=128)
                        w2v = ffn_w2[l, fi].rearrange("(fc p) n -> p fc n", p=128)
                        r1.units += [w1v[:, :, fb * 256:(fb + 1) * 256] for fb in range(22)]
                        r3.units += [w3v[:, :, fb * 256:(fb + 1) * 256] for fb in range(22)]
                        r2.units += [w2v[:, :, n * 128:(n + 1) * 128] for n in range(KC)]
                        continue
                    r1.units += [W1c[f, fb] for fb in range(22)]
                    r3.units += [W3c[f, fb] for fb in range(22)]
                    r2.units += [W2c[f, n] for n in range(KC)]
                if f == 0:
                    wc = lambda: kb.wait_all("pool", "conv")
                    r1.pre[22] = wc
                    r3.pre[22] = wc
                    r2.pre[KC] = wc
                conv_emit(1000)
                if f != 0:
                    kb.wait_all("pool", "conv")
                r1.issue_upto(2)
                r3.issue_upto(2)
                r2.issue_upto(3)
                emit_modulate(l, j, blocks[0], xs, hT, "hT")
                tail = []
                for g, blk in enumerate(blocks):
                    for fc in range(FC):
                        if fc >= 1:
                            for _ in range(3):
                                if tail:
                                    tail.pop(0)()
                        u = g * 22 + fc // 2
                        fl = fc % 2
                        for (ring, pbank, nm) in ((r1, pu1, "pu1"), (r3, pu3, "pu3")):
                            wb = ring.buf(u)
                            pb = pbank[fc % 2]
                            for kc in range(KC):
                                kb.op("pe", lambda e, wb=wb, pb=pb, kc=kc, fl=fl: e.matmul(
                                    pb[:], lhsT=wb[:, kc, fl * 128:(fl + 1) * 128], rhs=hT[:, kc, :],
                                    start=(kc == 0), stop=(kc == KC - 1)),
                                    r=[ring.key(u), ("hT", kc)], w=[(nm, fc % 2)], inc=(kc == KC - 1))
                        if fl == 1:
                            r1.issue_upto(u + 3)
                            r3.issue_upto(u + 3)
                        sb_ = sil[fc % 2]
                        kb.op("act", lambda e, sb_=sb_, fc=fc: e.activation(out=sb_[:], in_=pu1[fc % 2][:], func=AF.Silu),
                              r=[("pu1", fc % 2)], w=[("sil", fc % 2)])
                        kb.op("dve", lambda e, sb_=sb_, fc=fc: e.tensor_tensor(
                            out=uT[:, fc, :], in0=pu3[fc % 2][:], in1=sb_[:], op=ALU.mult),
                            r=[("pu3", fc % 2), ("sil", fc % 2)], w=[("uT", fc)])
                    nxt = None
                    if g + 1 < ng:
                        nxt = lambda g=g: emit_modulate(l, j, blocks[g + 1], xs, hT, "hT")
                    while tail:
                        tail.pop(0)()
                    tail = emit_outproj_epilogue(l, j, blk, FC, uT, "uT", r2, g * KC, xz, modh, l * 3 + j, py, pst,
                                                 tmp, after_load=nxt, defer=True)
                while tail:
                    tail.pop(0)()

        NT = 18

        def tile_of(blk, i):
            if blk == 0:
                return i // 2, i % 2, False, 0
            b = 0 if blk <= 4 else 1
            t = ((blk - 1) % 4) * 4 + i
            return b, 2 + t, True, t

        def ot_dst(b, t):
            if t < 2:
                return 0, (b * 2 + t) * 128
            return 1 + 4 * b + (t - 2) // 4, ((t - 2) % 4) * 128

        def evac(i, dst, src, r, w):
            if i % 2 == 0:
                kb.op("dve", lambda e: e.tensor_copy(out=dst, in_=src), r=r, w=w)
            else:
                kb.op("act", lambda e: e.activation(out=dst, in_=src, func=AF.Copy), r=r, w=w)

        class TrCtx:
            def __init__(self, ptr, eng=None):
                self.ptr = ptr
                self.n = 0
                self.eng = eng

            def run(self, srcs, dst_t, base, rkeys, wkey):
                for j0 in range(0, len(srcs), 8):
                    grp = srcs[j0:j0 + 8]
                    n_ = len(grp)
                    bi = self.n % len(self.ptr)
                    self.n += 1
                    pb = self.ptr[bi]
                    w_ = grp[0].shape[-1]
                    for jj, s in enumerate(grp):
                        kb.op("pe", lambda e, s=s, jj=jj, pb=pb: e.transpose(
                            out=pb[0:w_, jj * 128:(jj + 1) * 128], in_=s, identity=identB[:]),
                            r=list(rkeys) + ["identB"], w=[("ptr", bi)], inc=(jj == n_ - 1))
                    src = pb[0:w_, 0:n_ * 128].rearrange("p (a b) -> p a b", a=n_)
                    dst = dst_t[0:w_, base + j0:base + j0 + n_, :]
                    evac(1 if self.eng == "act" else self.n, dst, src, [("ptr", bi)], [wkey])

        def rms_scale(ss, n, tmpv, extra=1.0):
            kb.op("dve", lambda e: e.tensor_scalar(out=ss, in0=ss, scalar1=1.0 / n, scalar2=EPS, op0=ALU.mult,
                                                   op1=ALU.add), r=[tmpv], w=[tmpv])
            kb.op("act", lambda e: e.activation(out=ss, in_=ss, func=AF.Sqrt), r=[tmpv], w=[tmpv])
            kb.op("dve", lambda e: e.reciprocal(out=ss, in_=ss), r=[tmpv], w=[tmpv])
            if extra != 1.0:
                kb.op("dve", lambda e: e.tensor_scalar_mul(out=ss, in0=ss, scalar1=extra), r=[tmpv], w=[tmpv])

        def rope_cast(dst, src, cos, sin, lat, tmps, rk, wk):
            if not lat:
                kb.op("dve", lambda e: e.tensor_copy(out=dst, in_=src), r=rk, w=[wk])
                return
            h = src.shape[-1] // 2
            H = src.shape[1]
            t1, t2 = [t[:, 0:H * h].rearrange("p (a b) -> p a b", a=H) for t in tmps]
            x1, x2 = src[:, :, 0:h], src[:, :, h:2 * h]
            kb.op("dve", lambda e: e.tensor_tensor(out=t1, in0=x1, in1=cos, op=ALU.mult), r=rk, w=["rt1"])
            kb.op("dve", lambda e: e.tensor_tensor(out=t2, in0=x2, in1=sin, op=ALU.mult), r=rk, w=["rt2"])
            kb.op("dve", lambda e: e.tensor_tensor(out=dst[:, :, 0:h], in0=t1, in1=t2, op=ALU.subtract),
                  r=["rt1", "rt2"], w=[wk])
            kb.op("dve", lambda e: e.tensor_tensor(out=t1, in0=x2, in1=cos, op=ALU.mult), r=rk, w=["rt1"])
            kb.op("dve", lambda e: e.tensor_tensor(out=t2, in0=x1, in1=sin, op=ALU.mult), r=rk, w=["rt2"])
            kb.op("dve", lambda e: e.tensor_tensor(out=dst[:, :, h:2 * h], in0=t1, in1=t2, op=ALU.add),
                  r=["rt1", "rt2"], w=[wk])

        def stage_attn_a(l):
            with kb.scope() as st:
                xs = kb.sb(st, "xs", [128, 8, TB], F32)
                hT = kb.sb(st, "hT", [128, KC, TB], BF16)
                wslots = [kb.sb(st, "wi", [128, KC, 512], BF16) for _ in range(3)]
                ring = Ring(kb, "rw", wslots)
                zw = 1024 if l == 0 else 2048
                zsets = [[kb.sb(st, "zsec", [128, zw], F32) for _ in range(4)] for _ in range(2)]
                wbf = kb.sb(st, "wbf", [128, 2048], BF16)
                rt = [kb.sb(st, "rt", [128, 1024], F32) for _ in range(2)]
                gw = 512 if l == 0 else 1024
                ropeGs = [kb.sb(st, "ropeG", [128, 2, gw], F32) for _ in range(2)]
                ropeMs = [kb.sb(st, "ropeM", [128, 2, 256], F32) for _ in range(2)]
                trs = kb.sb(st, "trs", [128, 16, 128], BF16)
                trs2 = kb.sb(st, "trs2", [128, 9, 128], BF16)
                ssv = kb.sb(st, "ssv", [128, 16], F32)
                mhalf = kb.sb(st, "mhalf", [128, 16], F32)
                sqj = kb.sb(st, "sqj", [128, 1024], F32)
                pz = [kb.ps(st, "pz", [128, 512], F32) for _ in range(3)]
                ptr = [kb.ps(st, "ptr", [128, 1024], BF16) for _ in range(2)]
                tr = TrCtx(ptr, eng="act")
                kb.op("pool", lambda e: e.memset(mhalf[:], -0.5), w=["mhalf"])
                if l == 0:
                    wuq = kb.sb(st, "wuq", [128, 4, 1536], BF16)
                    wukv = kb.sb(st, "wukv", [128, 2, 2048], BF16)
                    gt = kb.sb(st, "gt", [128, 1024], F32)
                    c6 = kb.sb(st, "c6", [128, 6, 128], BF16)
                    qm = kb.sb(st, "qm", [128, 1536], F32)
                    kvm = kb.sb(st, "kvm", [128, 2048], F32)
                    kb.dma("pool", wuq[:], mla_w_uq.rearrange("(kc p) n -> p kc n", p=128), w=["wuq"])
                    kb.dma("pool", wukv[:], mla_w_ukv.rearrange("(kc p) n -> p kc n", p=128), w=["wukv"])
                    kb.dma("sp", gt[:], g0_in, w=["gt"])
                    wv = mg_w_in.rearrange("(kc p) n -> p kc n", p=128)
                    secs = [[(0, 512), (512, 320)], [(832, 512), (1344, 512)], [(1856, 512)]]
                else:
                    wv = diff_w_in.rearrange("(kc p) n -> p kc n", p=128)
                    secs = [[(s * 2048 + c * 512, 512) for c in range(4)] for s in range(3)]
                blocks = list(range(NBLK))
                cn = dict(pz=0)

                def secs_of(blk):
                    if l == 1 and blk == 0:
                        return [1, 2]
                    return [0, 1, 2]

                def act_evac(dst, src, r, w):
                    kb.op("act", lambda e: e.activation(out=dst, in_=src, func=AF.Copy), r=r, w=w)

                def rms_pow(ss, n, key):
                    k = ss.shape[-1]
                    kb.op("dve", lambda e: e.tensor_scalar(out=ss, in0=ss, scalar1=1.0 / n, scalar2=EPS, op0=ALU.mult,
                                                           op1=ALU.add), r=[key], w=[key])
                    kb.op("pool", lambda e: e.tensor_tensor(out=ss, in0=ss, in1=mhalf[:, 0:k], op=ALU.pow),
                          r=[key, "mhalf"], w=[key])

                def make_post(blk, s, i, zs, zp):
                    b, t, lat, pt = tile_of(blk, i)
                    z = zs[i]
                    zk = [("zsec", zp, i)]
                    qdst = QT_all[b, t]
                    kdst = KT_all[b]
                    kcols = slice(t * 128, (t + 1) * 128)
                    th = []
                    T = th.append
                    ropeG = ropeGs[i % 2]
                    ropeM = ropeMs[i % 2]
                    rgk = ("ropeG", i % 2)
                    rmk = ("ropeM", i % 2)

                    def rope_load(ii):
                        b_, t_, lat_, pt_ = tile_of(blk, ii)
                        if not lat_ or (l == 1 and s == 2):
                            return
                        if l == 0 and s == 0:
                            kb.dma("sp", ropeMs[ii % 2][:], ropeM_in[pt_ * 128:(pt_ + 1) * 128], w=[("ropeM", ii % 2)],
                                   sem=("ld_ropeM", ii % 2))
                        else:
                            kb.dma("sp", ropeGs[ii % 2][:], ropeG_in[pt_ * 128:(pt_ + 1) * 128, :, 0:gw],
                                   w=[("ropeG", ii % 2)], sem=("ld_ropeG", ii % 2))
                    if i == 0:
                        rope_load(0)
                    if i < 3:
                        T(lambda: rope_load(i + 1))
                    if l == 1:
                        if s == 2:
                            T(lambda: kb.op("act", lambda e: e.activation(out=wbf[:], in_=z[:], func=AF.Copy), r=zk, w=["wbf"]))
                            T(lambda: kb.dma("sp", V_all[b, t], wbf[:], r=["wbf"], w=[("V", b, t)], sem="st_wbf"))
                            return th
                        w3 = wbf[:].rearrange("p (a b) -> p a b", a=16)
                        T(lambda: rope_cast(w3, z[:].rearrange("p (a b) -> p a b", a=16),
                                            ropeG[:, 0, :].rearrange("p (a b) -> p a b", a=16),
                                            ropeG[:, 1, :].rearrange("p (a b) -> p a b", a=16), lat, rt,
                                            zk + [rgk], "wbf"))
                        T(lambda: tr.run([wbf[:, h * 128:(h + 1) * 128] for h in range(8)], trs, 0, ["wbf"], "trs"))
                        T(lambda: tr.run([wbf[:, h * 128:(h + 1) * 128] for h in range(8, 16)], trs, 8, ["wbf"], "trs"))
                        if s == 0:
                            T(lambda: kb.dma("sp", qdst[:, 0:16, :], trs[:], r=["trs"], w=[("Q", b, t)], sem="st_trs"))
                        else:
                            T(lambda: kb.dma("sp", kdst[:, 0:16, kcols], trs[:], r=["trs"], w=[("K", b, t)],
                                             sem="st_trs"))
                        return th
                    if s == 0:
                        def n1():
                            kb.op("act", lambda e: e.activation(out=sqj[:, 0:512], in_=z[:, 0:512], func=AF.Square,
                                                                accum_out=ssv[:, 0:1]), r=zk, w=["sqj", "ssv"])
                            kb.op("act", lambda e: e.activation(out=sqj[:, 512:768], in_=z[:, 512:768], func=AF.Square,
                                                                accum_out=ssv[:, 1:2]), r=zk, w=["sqj", "ssv"])
                            kb.op("dve", lambda e: e.tensor_scalar_mul(out=ssv[:, 1:2], in0=ssv[:, 1:2], scalar1=2.0),
                                  r=["ssv"], w=["ssv"])
                            rms_pow(ssv[:, 0:2], 512.0, "ssv")
                        T(n1)

                        def n2():
                            kb.op("dve", lambda e: e.scalar_tensor_tensor(
                                out=wbf[:, 0:512], in0=z[:, 0:512], scalar=ssv[:, 0:1], in1=gt[:, 0:512],
                                op0=ALU.mult, op1=ALU.mult), r=zk + ["ssv", "gt"], w=["wbf"])
                            kb.op("dve", lambda e: e.scalar_tensor_tensor(
                                out=wbf[:, 512:768], in0=z[:, 512:768], scalar=ssv[:, 1:2], in1=gt[:, 512:768],
                                op0=ALU.mult, op1=ALU.mult), r=zk + ["ssv", "gt"], w=["wbf"])
                        T(n2)
                        T(lambda: tr.run([wbf[:, h * 128:(h + 1) * 128] for h in range(6)], c6, 0, ["wbf"], "c6"))

                        def mmq(c):
                            pb = pz[cn["pz"] % 3]
                            pk = ("pz", cn["pz"] % 3)
                            cn["pz"] += 1
                            for kc in range(4):
                                kb.op("pe", lambda e, kc=kc: e.matmul(
                                    pb[:], lhsT=c6[:, kc, :], rhs=wuq[:, kc, c * 512:(c + 1) * 512],
                                    start=(kc == 0), stop=(kc == 3)), r=["c6", "wuq"], w=[pk], inc=(kc == 3))
                            act_evac(qm[:, c * 512:(c + 1) * 512], pb[:], [pk], ["qm"])

                        def mmkv(c):
                            pb = pz[cn["pz"] % 3]
                            pk = ("pz", cn["pz"] % 3)
                            cn["pz"] += 1
                            for kc in range(2):
                                kb.op("pe", lambda e, kc=kc: e.matmul(
                                    pb[:], lhsT=c6[:, 4 + kc, :], rhs=wukv[:, kc, c * 512:(c + 1) * 512],
                                    start=(kc == 0), stop=(kc == 1)), r=["c6", "wukv"], w=[pk], inc=(kc == 1))
                            act_evac(kvm[:, c * 512:(c + 1) * 512], pb[:], [pk], ["kvm"])
                        for c in range(3):
                            T(lambda c=c: mmq(c))
                        for c in range(4):
                            T(lambda c=c: mmkv(c))
                        qm3 = qm[:].rearrange("p (a b) -> p a b", a=8)
                        kv3 = kvm[:].rearrange("p (a b) -> p a b", a=8)
                        T(lambda: kb.op("act", lambda e: e.activation(
                            out=wbf[:, 0:1024].rearrange("p (a b) -> p a b", a=8), in_=qm3[:, :, 0:128], func=AF.Copy),
                            r=["qm"], w=["wbf"]))
                        T(lambda: rope_cast(wbf[:, 1024:1536].rearrange("p (a b) -> p a b", a=8), qm3[:, :, 128:192],
                                            ropeM[:, 0, :].rearrange("p (a b) -> p a b", a=8),
                                            ropeM[:, 1, :].rearrange("p (a b) -> p a b", a=8), lat, rt,
                                            ["qm", rmk], "wbf"))
                        T(lambda: rope_cast(wbf[:, 1536:1600].rearrange("p (a b) -> p a b", a=1),
                                            z[:, 768:832].rearrange("p (a b) -> p a b", a=1),
                                            ropeM[:, 0, 0:32].rearrange("p (a b) -> p a b", a=1),
                                            ropeM[:, 1, 0:32].rearrange("p (a b) -> p a b", a=1), lat, rt,
                                            zk + [rmk], "wbf"))
                        T(lambda: tr.run([wbf[:, h * 128:(h + 1) * 128] for h in range(8)], trs, 0, ["wbf"], "trs"))
                        T(lambda: tr.run([wbf[:, 1024 + h * 64:1024 + (h + 1) * 64] for h in range(8)], trs, 8,
                                         ["wbf"], "trs"))

                        def st1():
                            kb.dma("sp", qdst[:, 0:8, :], trs[:, 0:8, :], r=["trs"], w=[("Q", b, t)], sem="st_trs")
                            kb.dma("sp", qdst[0:64, 16:24, :], trs[0:64, 8:16, :], r=["trs"], w=[("Q", b, t)],
                                   sem="st_trs")
                        T(st1)
                        T(lambda: tr.run([wbf[:, 1536:1600]], trs2, 8, ["wbf"], "trs2"))
                        T(lambda: kb.op("act", lambda e: e.activation(
                            out=wbf[:, 0:1024].rearrange("p (a b) -> p a b", a=8), in_=kv3[:, :, 0:128], func=AF.Copy),
                            r=["kvm"], w=["wbf"]))
                        T(lambda: kb.op("dve", lambda e: e.tensor_copy(
                            out=wbf[:, 1024:2048].rearrange("p (a b) -> p a b", a=8), in_=kv3[:, :, 128:256]),
                            r=["kvm"], w=["wbf"]))
                        T(lambda: tr.run([wbf[:, h * 128:(h + 1) * 128] for h in range(8)], trs2, 0, ["wbf"], "trs2"))

                        def st2():
                            kb.dma("sp", kdst[:, 0:8, kcols], trs2[:, 0:8, :], r=["trs2"], w=[("K", b, t)], sem="st_trs2")
                            kb.dma("sp", kdst[0:64, 10, kcols], trs2[0:64, 8, :], r=["trs2"], w=[("K", b, t)],
                                   sem="st_trs2")
                            kb.dma("sp", V_all[b, t][:, 0:1024], wbf[:, 1024:2048], r=["wbf"], w=[("V", b, t)],
                                   sem="st_wbf")
                        T(st2)
                        return th
                    nh = 8 if s == 1 else 2
                    gsl = gt[:, 768:896] if s == 1 else gt[:, 896:1024]
                    z3 = z[:, 0:nh * 128].rearrange("p (a b) -> p a b", a=nh)

                    def g1():
                        kb.op("act", lambda e: e.activation(out=sqj[:, 0:nh * 128], in_=z[:, 0:nh * 128],
                                                            func=AF.Square), r=zk, w=["sqj"])
                        kb.op("dve", lambda e: e.tensor_reduce(
                            out=ssv[:, 0:nh], in_=sqj[:, 0:nh * 128].rearrange("p (a b) -> p a b", a=nh),
                            axis=AX.X, op=ALU.add), r=["sqj"], w=["ssv"])
                        rms_pow(ssv[:, 0:nh], 128.0, "ssv")
                    T(g1)

                    def g2():
                        for h in range(nh):
                            kb.op("dve", lambda e, h=h: e.scalar_tensor_tensor(
                                out=z[:, h * 128:(h + 1) * 128], in0=z[:, h * 128:(h + 1) * 128],
                                scalar=ssv[:, h:h + 1], in1=gsl, op0=ALU.mult, op1=ALU.mult),
                                r=["ssv", "gt"], w=zk)
                    T(g2)
                    T(lambda: rope_cast(wbf[:, 0:nh * 128].rearrange("p (a b) -> p a b", a=nh), z3,
                                        ropeG[:, 0, 0:nh * 64].rearrange("p (a b) -> p a b", a=nh),
                                        ropeG[:, 1, 0:nh * 64].rearrange("p (a b) -> p a b", a=nh), lat, rt,
                                        zk + [rgk], "wbf"))
                    T(lambda: tr.run([wbf[:, h * 128:(h + 1) * 128] for h in range(nh)], trs, 0, ["wbf"], "trs"))
                    if s == 1:
                        T(lambda: kb.dma("sp", qdst[:, 8:16, :], trs[:, 0:8, :], r=["trs"], w=[("Q", b, t)], sem="st_trs"))
                    else:
                        def st3():
                            kb.dma("sp", kdst[:, 8:10, kcols], trs[:, 0:2, :], r=["trs"], w=[("K", b, t)], sem="st_trs")
                            kb.op("act", lambda e: e.activation(out=wbf[:, 1024:1280], in_=z[:, 256:512], func=AF.Copy),
                                  r=zk, w=["wbf"])
                            kb.dma("sp", V_all[b, t][:, 1024:1280], wbf[:, 1024:1280], r=["wbf"], w=[("V", b, t)],
                                   sem="st_wbf")
                        T(st3)
                    return th

                for blk in blocks:
                    for s in secs_of(blk):
                        ring.units += [wv[:, :, c0:c0 + w_] for (c0, w_) in secs[s]]
                ring.issue_upto(3)
                uidx = 0
                if l == 0:
                    for f_ in (1, 2, 3):
                        conv_pending.extend(conv_jobs(f_))
                pending = []
                zp = 0
                for blk in blocks:
                    emit_modulate(l, 1, blk, xs, hT, "hT")
                    for s in secs_of(blk):
                        zs = zsets[zp]
                        off = 0
                        nslots = len(secs[s]) * 4
                        per = (len(pending) + nslots - 1) // nslots if pending else 0
                        for (c0, w_) in secs[s]:
                            wb = ring.buf(uidx)
                            for i in range(4):
                                pb = pz[cn["pz"] % 3]
                                pk = ("pz", cn["pz"] % 3)
                                cn["pz"] += 1
                                for kc in range(KC):
                                    kb.op("pe", lambda e, pb=pb, wb=wb, kc=kc, i=i, w_=w_: e.matmul(
                                        pb[:, 0:w_], lhsT=hT[:, kc, i * 128:(i + 1) * 128], rhs=wb[:, kc, 0:w_],
                                        start=(kc == 0), stop=(kc == KC - 1)),
                                        r=[ring.key(uidx), ("hT", kc)], w=[pk], inc=(kc == KC - 1))
                                act_evac(zs[i][:, off:off + w_], pb[:, 0:w_], [pk], [("zsec", zp, i)])
                                conv_emit(1)
                                for _ in range(per):
                                    if pending:
                                        pending.pop(0)()
                            ring.issue_upto(uidx + 4)
                            uidx += 1
                            off += w_
                        while pending:
                            pending.pop(0)()
                        for i in range(4):
                            pending += make_post(blk, s, i, zs, zp)
                        zp ^= 1
                while pending:
                    pending.pop(0)()

        def stage_attn_b(l, host_mod=None):
            SHIFT = 10.0
            if l == 0:
                groups = [dict(ks=list(range(8)) + [10], vc0=0, nvs=8, dv=128, kc0=0, scale=192.0 ** -0.5, mla=True,
                               q0=0),
                          dict(ks=[8, 9], vc0=1024, nvs=2, dv=128, kc0=8, scale=128.0 ** -0.5, mla=False, q0=8)]
            else:
                groups = [dict(ks=list(range(8 * g, 8 * g + 8)), vc0=1024 * g, nvs=4, dv=256, kc0=8 * g,
                               scale=128.0 ** -0.5, mla=False, q0=8 * g) for g in range(2)]
            diff = (l == 1)
            with kb.scope() as st:
                KTs = kb.sb(st, "KTs", [128, 9, NT * 128], BF16)
                Vaf = kb.sb(st, "Vaf", [128, NT * 1032], BF16)
                QTs = [kb.sb(st, "QTs", [128, 16, 128], BF16) for _ in range(3)]
                NPT = 8
                PTr = [kb.sb(st, "PTr", [128, 4, 128], BF16) for _ in range(NPT)]
                Obfs = [kb.sb(st, "Obf", [128, 1024], BF16) for _ in range(2)]
                OTs = kb.sb(st, "OTs", [128, 8, 128], BF16)
                sv = kb.sb(st, "sv", [128, 8], F32)
                negC = kb.sb(st, "negC", [128, 1], F32)
                O0n = kb.sb(st, "O0n", [128, 256], F32)
                O32 = kb.sb(st, "O32", [128, 256], F32)
                sqj = kb.sb(st, "sqj", [128, 256], F32)
                NSR = 5 if host_mod is None else 4
                Sr = [kb.ps(st, "Sr", [128, 512], F32) for _ in range(NSR)]
                hosted = []
                if host_mod is not None:
                    pm_ = kb.ps(st, "pm", [128, 512], F32)
                    hosted = mod_thunks(host_mod, st, pm_)
                Op = [kb.ps(st, "Op", [128, 512], F32) for _ in range(2)]
                ptr = [kb.ps(st, "ptr", [128, 1024], BF16)]
                tr = TrCtx(ptr)
                kb.op("pool", lambda e: e.memset(negC[:], -SHIFT), w=["negC"])
                mhalfB = kb.sb(st, "mhalfB", [128, 1], F32)
                kb.op("pool", lambda e: e.memset(mhalfB[:], -0.5), w=["mhalfB"])
                if l == 0:
                    kb.op("pool", lambda e: e.memset(KTs[64:128, 8, :], 0.0), w=[("KTs", 8)])
                    for qi in range(3):
                        kb.op("pool", lambda e, qi=qi: e.memset(QTs[qi][64:128, 8:16, :], 0.0), w=[("QTs", qi)])
                if diff:
                    lamv = kb.sb(st, "lamv", [128, 4], F32)
                    lqk = kb.sb(st, "lqk", [128, 4, 128], F32)
                    gsub = kb.sb(st, "gsub", [128, 256], F32)
                    kb.dma("sp", lqk[:], lqk_in, w=["lqk"])
                    kb.dma("sp", gsub[:], gsub_in, w=["gsub"])
                    kb.op("dve", lambda e: e.tensor_tensor(out=lqk[:, 0, :], in0=lqk[:, 0, :], in1=lqk[:, 1, :],
                                                           op=ALU.mult), r=["lqk"], w=["lqk"])
                    kb.op("dve", lambda e: e.tensor_tensor(out=lqk[:, 2, :], in0=lqk[:, 2, :], in1=lqk[:, 3, :],
                                                           op=ALU.mult), r=["lqk"], w=["lqk"])
                    kb.op("dve", lambda e: e.reduce_sum(out=lamv[:, 0:1], in_=lqk[:, 0, :], axis=AX.X), r=["lqk"],
                          w=["lamv"])
                    kb.op("dve", lambda e: e.reduce_sum(out=lamv[:, 1:2], in_=lqk[:, 2, :], axis=AX.X), r=["lqk"],
                          w=["lamv"])
                    kb.op("act", lambda e: e.activation(out=lamv[:, 0:2], in_=lamv[:, 0:2], func=AF.Exp),
                          r=["lamv"], w=["lamv"])
                    kb.op("dve", lambda e: e.tensor_tensor(out=lamv[:, 2:3], in0=lamv[:, 1:2], in1=lamv[:, 0:1],
                                                           op=ALU.subtract), r=["lamv"], w=["lamv"])
                    kb.op("dve", lambda e: e.tensor_scalar_add(out=lamv[:, 2:3], in0=lamv[:, 2:3],
                                                               scalar1=-LAMBDA_INIT1), r=["lamv"], w=["lamv"])
                cnt = dict(s=0, p=0, q=0, u=0, o=0)
                for b in range(2):
                    for gi, g in enumerate(groups):
                        dv, nvs = g["dv"], g["nvs"]
                        dvp = dv + 1
                        Va = Vaf[:, 0:NT * nvs * dvp].rearrange("p (t h d) -> p t h d", t=NT, h=nvs)
                        for si, ks in enumerate(g["ks"]):
                            npart = 64 if (g["mla"] and ks == 10) else 128
                            kb.dma("sp", KTs[0:npart, si, :], KT_all[b][0:npart, ks, :], w=[("KTs", si)], sem="ld_kv")
                        kb.op("pool", lambda e, Va=Va, dv=dv: e.memset(Va[:, :, :, dv:dv + 1], 1.0), w=["Va"])
                        for hs in range(nvs):
                            c0 = g["vc0"] + hs * dv
                            kb.dma("sp", Va[:, :, hs, 0:dv], V_all[b][:, :, c0:c0 + dv].rearrange("t p d -> p t d"),
                                   w=["Va"], sem="ld_kv")
                        qtiles = list(range(NT)) if l == 0 else list(range(2, NT))
                        nq = len(g["ks"]) if False else 8

                        def load_q(t, g=g, b=b):
                            qi_ = cnt["q"] % 3
                            QT = QTs[qi_]
                            qkey = ("QTs", qi_)
                            qsem = ("ld_qt", qi_)
                            cnt["q"] += 1
                            if g["mla"]:
                                kb.dma("sp", QT[:, 0:8, :], QT_all[b, t][:, 0:8, :], w=[qkey], sem=qsem)
                                kb.dma("sp", QT[0:64, 8:16, :], QT_all[b, t][0:64, 16:24, :], w=[qkey], sem=qsem)
                            else:
                                kb.dma("sp", QT[:, 0:8, :], QT_all[b, t][:, g["q0"]:g["q0"] + 8, :], w=[qkey], sem=qsem)
                            return QT, qkey

                        def chunks_of(t):
                            nkt = 2 if t < 2 else NT
                            return [(kt0, min(4, nkt - kt0)) for kt0 in range(0, nkt, 4)]

                        qinfo = {}
                        qinfo[qtiles[0]] = load_q(qtiles[0])
                        units = [(t, h) for t in qtiles for h in range(8)]
                        pend = {}

                        def emit_S(ui, ci, g=g):
                            t, h = units[ui]
                            QT, qkey = qinfo[t]
                            kt0, nk = chunks_of(t)[ci]
                            bi = cnt["s"] % NSR
                            cnt["s"] += 1
                            bank = Sr[bi]
                            kslot = (h // 4) if (l == 0 and not g["mla"]) else h
                            for j in range(nk):
                                kt = kt0 + j
                                parts = [(KTs[:, kslot, kt * 128:(kt + 1) * 128], QT[:, h, :])]
                                if g["mla"]:
                                    parts.append((KTs[:, 8, kt * 128:(kt + 1) * 128], QT[:, 8 + h, :]))
                                for pi, (ka, qa) in enumerate(parts):
                                    kb.op("pe", lambda e, ka=ka, qa=qa, j=j, pi=pi, np_=len(parts), bank=bank: e.matmul(
                                        bank[:, j * 128:(j + 1) * 128], lhsT=ka, rhs=qa, start=(pi == 0),
                                        stop=(pi == np_ - 1)),
                                        r=[qkey, ("KTs", kslot)] + ([("KTs", 8)] if g["mla"] else []),
                                        w=[("Sr", bi)], inc=(j == nk - 1 and pi == len(parts) - 1))
                            pi_ = cnt["p"] % NPT
                            cnt["p"] += 1
                            ptb = PTr[pi_]
                            kb.op("act", lambda e, ptb=ptb, bank=bank, nk=nk: e.activation(
                                out=ptb[:, 0:nk, :], in_=bank[:, 0:nk * 128].rearrange("p (a b) -> p a b", a=nk),
                                func=AF.Exp, scale=g["scale"], bias=negC[:, 0:1]),
                                r=[("Sr", bi), "negC"], w=[("PTr", pi_)])
                            pend[(ui, ci)] = (ptb, ("PTr", pi_))

                        def emit_PV(ui, ci, g=g, Va=Va, dvp=dvp):
                            t, h = units[ui]
                            ch = chunks_of(t)
                            kt0, nk = ch[ci]
                            ptb, pkey = pend.pop((ui, ci))
                            ob = ui % 2
                            if diff:
                                vs = h // 2
                            elif g["mla"]:
                                vs = h
                            else:
                                vs = h // 4
                            for j in range(nk):
                                kt = kt0 + j
                                first = (ci == 0 and j == 0)
                                last = (ci == len(ch) - 1 and j == nk - 1)
                                kb.op("pe", lambda e, ptb=ptb, j=j, kt=kt, first=first, last=last, ob=ob, vs=vs: e.matmul(
                                    Op[ob][:, 0:dvp], lhsT=ptb[:, j, :], rhs=Va[:, kt, vs, :], start=first, stop=last),
                                    r=[pkey, "Va"], w=[("Op", ob)], inc=(j == nk - 1))

                        def finalize(ui, g=g, dv=dv, b=b):
                            t, h = units[ui]
                            ob = ui % 2
                            Obf = Obfs[(ui // 8) % 2]
                            okey = ("Obf", (ui // 8) % 2)
                            O = Op[ob]
                            if not diff:
                                kb.op("dve", lambda e: e.reciprocal(out=sv[:, 0:1], in_=O[:, dv:dv + 1]),
                                      r=[("Op", ob)], w=["sv0"])
                                kb.op("dve", lambda e: e.tensor_scalar(
                                    out=Obf[:, h * 128:(h + 1) * 128], in0=O[:, 0:128], scalar1=sv[:, 0:1],
                                    scalar2=None, op0=ALU.mult), r=[("Op", ob), "sv0"], w=[okey])
                            elif h % 2 == 0:
                                kb.op("dve", lambda e: e.reciprocal(out=sv[:, 0:1], in_=O[:, dv:dv + 1]),
                                      r=[("Op", ob)], w=["sv0"])
                                kb.op("dve", lambda e: e.tensor_scalar(
                                    out=O0n[:], in0=O[:, 0:256], scalar1=sv[:, 0:1], scalar2=None, op0=ALU.mult),
                                    r=[("Op", ob), "sv0"], w=["O0n"])
                            else:
                                hh = h // 2
                                kb.op("dve", lambda e: e.reciprocal(out=sv[:, 1:2], in_=O[:, dv:dv + 1]),
                                      r=[("Op", ob)], w=["sv1"])
                                kb.op("dve", lambda e: e.tensor_tensor(out=sv[:, 1:2], in0=sv[:, 1:2], in1=lamv[:, 2:3],
                                                                       op=ALU.mult), r=["sv1", "lamv"], w=["sv1"])
                                kb.op("dve", lambda e: e.scalar_tensor_tensor(
                                    out=O32[:], in0=O[:, 0:256], scalar=sv[:, 1:2], in1=O0n[:], op0=ALU.mult,
                                    op1=ALU.add), r=[("Op", ob), "sv1", "O0n"], w=["O32"])
                                kb.op("dve", lambda e: e.tensor_tensor(out=sqj[:], in0=O32[:], in1=O32[:], op=ALU.mult),
                                      r=["O32"], w=["sqj"])
                                kb.op("dve", lambda e: e.reduce_sum(out=sv[:, 2:3], in_=sqj[:], axis=AX.X),
                                      r=["sqj"], w=["sv2"])
                                kb.op("dve", lambda e: e.tensor_scalar(out=sv[:, 2:3], in0=sv[:, 2:3], scalar1=1.0 / 256.0,
                                                                       scalar2=EPS, op0=ALU.mult, op1=ALU.add),
                                      r=["sv2"], w=["sv2"])
                                kb.op("pool", lambda e: e.tensor_tensor(out=sv[:, 2:3], in0=sv[:, 2:3], in1=mhalfB[:, 0:1],
                                                                        op=ALU.pow), r=["sv2", "mhalfB"], w=["sv2"])
                                kb.op("dve", lambda e: e.tensor_scalar_mul(out=sv[:, 2:3], in0=sv[:, 2:3],
                                                                           scalar1=1.0 - LAMBDA_INIT1),
                                      r=["sv2"], w=["sv2"])
                                kb.op("dve", lambda e, hh=hh: e.scalar_tensor_tensor(
                                    out=Obf[:, hh * 256:(hh + 1) * 256], in0=O32[:], scalar=sv[:, 2:3], in1=gsub[:],
                                    op0=ALU.mult, op1=ALU.mult), r=["O32", "sv2", "gsub"], w=[okey])
                            if h == 7:
                                if hosted:
                                    hosted.pop(0)()
                                tr.run([Obf[:, c * 128:(c + 1) * 128] for c in range(8)], OTs, 0, [okey], "OTs")
                                oblk, ocol = ot_dst(b, t)
                                kb.dma("sp", OT_all[oblk][:, g["kc0"]:g["kc0"] + 8, ocol:ocol + 128], OTs[:],
                                       r=["OTs"], w=[("OT", oblk)], sem="st_ots")

                        nu = len(units)
                        for ci in range(len(chunks_of(units[0][0]))):
                            emit_S(0, ci)
                        for ui in range(nu):
                            t, h = units[ui]
                            if h == 0:
                                ti = qtiles.index(t)
                                if ti + 1 < len(qtiles):
                                    qinfo[qtiles[ti + 1]] = load_q(qtiles[ti + 1])
                            nci = len(chunks_of(t))
                            ncn = len(chunks_of(units[ui + 1][0])) if ui + 1 < nu else 0
                            for ci in range(max(nci, ncn)):
                                if ci < ncn:
                                    emit_S(ui + 1, ci)
                                if ci < nci:
                                    emit_PV(ui, ci)
                            finalize(ui)
                while hosted:
                    hosted.pop(0)()

        def stage_attn_c(l, blocks):
            with kb.scope() as st:
                xzs = [kb.sb(st, "xz", [128, KC, TB], F32) for _ in range(2)]
                uTs = [kb.sb(st, "oT", [128, KC, TB], BF16) for _ in range(2)]
                tmp = alloc_epi_tmp(st)
                wos = [kb.sb(st, "wos", [128, KC, 128], BF16) for _ in range(KC)]
                py = [kb.ps(st, "py", [128, TB], F32) for _ in range(2)]
                pst = [kb.ps(st, "pst", [128, TB], F32) for _ in range(2)]
                ro = Ring(kb, "wo16", wos)
                wov = (mg_w_o if l == 0 else diff_w_o).rearrange("(fc p) n -> p fc n", p=128)
                ro.units += [wov[:, :, n * 128:(n + 1) * 128] for n in range(KC)]
                ro.issue_upto(KC)
                tail = []
                for g, blk in enumerate(blocks):
                    uT = uTs[g % 2]
                    ukey = "oT%d" % (g % 2)
                    kb.dma("sp", uT[:], OT_all[blk], r=[("OT", blk)], w=[(ukey, fc) for fc in range(KC)], sem=("ld_ot", g % 2))
                    tail = emit_outproj_epilogue(l, 1, blk, KC, uT, ukey, ro, g * KC, xzs[g % 2], modT, l * 3 + 1,
                                                 py, pst, tmp, xk="xz%d" % (g % 2), defer=True, prev_tail=tail)
                while tail:
                    tail.pop(0)()

        conv_pending.extend(conv_jobs(0))
        conv_emit(1000)
        stage_in()
        stage_silu()
        todo = stages if stages is not None else ["mod0", "ffn00", "attn0", "ffn02", "mod1", "ffn10", "attn1", "ffn12"]
        allb = list(range(NBLK))
        for s in todo:
            if s == "mod0":
                stage_mod(0)
            elif s == "mod1":
                if "attn0" not in todo:
                    stage_mod(1)
            elif s == "ffn00":
                stage_ffn(0, 0, 0, allb)
            elif s == "ffn02":
                stage_ffn(0, 2, 1, allb)
            elif s == "ffn10":
                stage_ffn(1, 0, 0, allb)
            elif s == "ffn12":
                stage_ffn(1, 2, 1, allb[1:])
            elif s == "attn0":
                stage_attn_a(0)
                stage_attn_b(0, host_mod=(1 if "mod1" in todo else None))
                stage_attn_c(0, allb)
            elif s == "attn1":
                stage_attn_a(1)
                stage_attn_b(1)
                stage_attn_c(1, allb[1:])
            elif s.startswith("ffnp"):
                stage_ffn(0, 0, 0, [int(c) for c in s[4:]])
        if dbg:
            with kb.scope() as st:
                for blk in range(NBLK):
                    kb.dma("sp", dbg_xt[blk], XT[blk], r=[("XT", blk)], w=[("dbg", blk)])
        stage_out()
        kb.barrier(skip=())
    return nc


_ROPE = None


def rope_tables():
    global _ROPE
    if _ROPE is None:
        pos = np.arange(SEQ)
        r = (pos // 64).astype(np.float32)[:, None]
        col = (pos % 64).astype(np.float32)[:, None]

        def tab(rot_dim, heads):
            nf = rot_dim // 4
            inv = (np.float32(10000.0) ** (-np.arange(nf, dtype=np.float32) / np.float32(nf))).astype(np.float32)
            ang = np.concatenate([r * inv, col * inv], -1).astype(np.float32)
            cs = np.stack([np.tile(np.cos(ang), (1, heads)), np.tile(np.sin(ang), (1, heads))], 1)
            return np.ascontiguousarray(cs, dtype=np.float32)
        _ROPE = (tab(64, 8), tab(128, 16))
    return _ROPE


def host_inputs(inputs, core):
    b0 = 2 * core
    f = lambda a: np.ascontiguousarray(a, dtype=np.float32)
    c3 = np.stack([inputs["c"][b0], inputs["c"][b0 + 1], inputs["c_ctx"]], 0)
    cT = c3.reshape(3, KC, 128).transpose(2, 1, 0)
    b_ada = inputs["b_ada"].reshape(2, NMOD, 128).transpose(0, 2, 1)
    b_adaT = np.repeat(b_ada[:, :, :, None], 3, axis=3).reshape(2, 128, NMOD * 3)
    lg = inputs["ln_g"].reshape(6, KC, 128).transpose(2, 0, 1)
    lb = inputs["ln_b"].reshape(6, KC, 128).transpose(2, 0, 1)
    rep = lambda v: np.broadcast_to(np.asarray(v, np.float32).reshape(1, -1), (128, np.asarray(v).size))
    g0 = np.concatenate([rep(inputs["mla_g_cq"][0]), rep(inputs["mla_g_ckv"][0]), rep(inputs["gqa_g_q"][0]),
                         rep(inputs["gqa_g_k"][0])], axis=1)
    lqk = np.stack([rep(inputs["diff_lq1"][0]), rep(inputs["diff_lk1"][0]), rep(inputs["diff_lq2"][0]),
                    rep(inputs["diff_lk2"][0])], axis=1)
    ropeM, ropeG = rope_tables()
    m = {
        "mg_w_in": f(inputs["mg_w_in"][0]), "mla_w_uq": f(inputs["mla_w_uq"][0]), "mla_w_ukv": f(inputs["mla_w_ukv"][0]),
        "mg_w_o": f(inputs["mg_w_o"][0]), "diff_w_in": f(inputs["diff_w_in"][0]), "diff_w_o": f(inputs["diff_w_o"][0]),
        "g0": f(g0), "ropeM": ropeM, "ropeG": ropeG, "lqk": f(lqk), "gsub": f(rep(inputs["diff_g_sub"][0])),
        "x": f(inputs["x"][b0:b0 + 2]), "ctx": f(inputs["ctx"][b0:b0 + 2]), "cT": f(cT),
        "w_ada": f(inputs["w_ada"]), "b_adaT": f(b_adaT), "ln_gT": f(lg), "ln_bT": f(lb),
        "ffn_w1": f(inputs["ffn_w1"]), "ffn_w3": f(inputs["ffn_w3"]), "ffn_w2": f(inputs["ffn_w2"]),
    }
    return m


def kernel(**inputs):
    nc = build_program()
    in_maps = [host_inputs(inputs, c) for c in range(NCORES)]
    res = run_bass_kernel_spmd(nc, in_maps, core_ids=list(range(NCORES)))
    return np.concatenate([np.asarray(r["out"]) for r in res.results], axis=0).astype(np.float32)
```

```python
import math
from contextlib import ExitStack, contextmanager
import numpy as np
import concourse.bass as bass
import concourse.mybir as mybir
from concourse.bass_utils import run_bass_kernel_spmd

F32 = mybir.dt.float32
BF16 = mybir.dt.bfloat16
ALU = mybir.AluOpType
AF = mybir.ActivationFunctionType
AX = mybir.AxisListType

NCORES = 8
D = 2048
KC = 16
FF = 5632
FC = 44
TB = 512
NBLK = 9
SEQ = 2048
CTX = 256
ALPHA = float((2 * 2) ** 0.25)
EPS = 1e-6
NMOD = 9 * KC
LAMBDA_INIT1 = float(0.8 - 0.6 * math.exp(-0.3 * 1))


class KB:
    def __init__(self, nc):
        self.nc = nc
        self.es = ExitStack()
        self.E = {"pe": nc.tensor, "act": nc.scalar, "dve": nc.vector, "pool": nc.gpsimd, "sp": nc.sync}
        self.sem, self.cnt = {}, {}
        self.waited = {e: {} for e in self.E}
        self.reg = {}
        self.uid = 0
        for e in ("pe", "act", "dve", "pool"):
            self.new_sem(e)
        self.new_sem("misc")
        self.new_sem("conv")
        self.shared = {"misc", "conv", "ld_kv"}

    def new_sem(self, key):
        self.sem[key] = self.es.enter_context(self.nc.semaphore("s_%s" % str(key).replace(" ", "")))
        self.cnt[key] = 0

    def _wait(self, eng, need):
        for k, v in need.items():
            if eng == "pe" and k == "pe":
                continue
            if k in self.shared:
                v = self.cnt[k]
            if self.waited[eng].get(k, 0) < v:
                self.E[eng].wait_ge(self.sem[k], v)
                self.waited[eng][k] = v

    def _deps(self, r, w):
        need = {}
        for key in r:
            ent = self.reg.get(key)
            if ent and ent[0]:
                k, v = ent[0]
                need[k] = max(need.get(k, 0), v)
        for key in w:
            ent = self.reg.get(key)
            if ent:
                if ent[0]:
                    k, v = ent[0]
                    need[k] = max(need.get(k, 0), v)
                for k, v in ent[1].items():
                    need[k] = max(need.get(k, 0), v)
        return need

    def _commit(self, tok, r, w):
        k, v = tok
        for key in r:
            ent = self.reg.setdefault(key, [None, {}])
            ent[1][k] = max(ent[1].get(k, 0), v)
        for key in w:
            self.reg[key] = [tok, {}]

    def op(self, eng, fn, r=(), w=(), inc=True):
        self._wait(eng, self._deps(r, w))
        ins = fn(self.E[eng])
        if inc:
            self.cnt[eng] += 1
            ins.then_inc(self.sem[eng], 1)
            tok = (eng, self.cnt[eng])
        else:
            tok = (eng, self.cnt[eng] + 1)
        self._commit(tok, r, w)
        return ins

    def dma(self, q, out, in_, r=(), w=(), sem="misc"):
        if sem not in self.sem:
            self.new_sem(sem)
        self._wait(q, self._deps(r, w))
        ins = self.E[q].dma_start(out=out, in_=in_)
        self.cnt[sem] += 16
        ins.then_inc(self.sem[sem], 16)
        self._commit((sem, self.cnt[sem]), r, w)

    def wait_all(self, e, k):
        v = self.cnt[k]
        if v > 0 and self.waited[e].get(k, 0) < v:
            self.E[e].wait_ge(self.sem[k], v)
            self.waited[e][k] = v

    def barrier(self, skip=("conv",)):
        for e in self.E:
            for k in self.sem:
                if k in skip:
                    continue
                v = self.cnt[k]
                if v > 0 and self.waited[e].get(k, 0) < v:
                    self.E[e].wait_ge(self.sem[k], v)
                    self.waited[e][k] = v
        self.reg = {}

    @contextmanager
    def scope(self):
        self.barrier()
        st = ExitStack()
        with st:
            yield st
            self.barrier()

    def name(self, base):
        self.uid += 1
        return "%s_%d" % (base, self.uid)

    def sb(self, st, base, shape, dt):
        return st.enter_context(self.nc.sbuf_tensor(self.name(base), list(shape), dt))

    def ps(self, st, base, shape, dt):
        return st.enter_context(self.nc.psum_tensor(self.name(base), list(shape), dt))


class Ring:
    def __init__(self, kb, name, slots, q="pool"):
        self.kb, self.name, self.slots, self.q = kb, name, slots, q
        self.units = []
        self.pre = {}
        self.issued = 0
        for s in range(len(slots)):
            if (name, s) not in kb.sem:
                kb.new_sem((name, s))

    def issue_upto(self, n):
        n = min(n, len(self.units))
        while self.issued < n:
            i = self.issued
            s = i % len(self.slots)
            if i in self.pre:
                self.pre[i]()
            un = self.units[i]
            dst = self.slots[s][:]
            if tuple(un.shape) != tuple(dst.shape):
                dst = self.slots[s][:, :, 0:un.shape[-1]]
            self.kb.dma(self.q, dst, un, w=[(self.name, s)], sem=(self.name, s))
            self.issued += 1

    def key(self, i):
        return (self.name, i % len(self.slots))

    def buf(self, i):
        return self.slots[i % len(self.slots)]


def build_program(dbg=False, stages=None):
    nc = bass.Bass("TRN2", target_bir_lowering=False)
    dt = lambda name, shape, d=F32: nc.dram_tensor(name, list(shape), d, kind="ExternalInput").ap()
    x_in = dt("x", [2, SEQ, D])
    ctx_in = dt("ctx", [2, CTX, D])
    cT_in = dt("cT", [128, KC, 3])
    w_ada = dt("w_ada", [2, D, 9 * D])
    b_adaT = dt("b_adaT", [2, 128, NMOD * 3])
    ln_gT = dt("ln_gT", [128, 6, KC])
    ln_bT = dt("ln_bT", [128, 6, KC])
    ffn_w1 = dt("ffn_w1", [2, 2, D, FF])
    ffn_w3 = dt("ffn_w3", [2, 2, D, FF])
    ffn_w2 = dt("ffn_w2", [2, 2, FF, D])
    mg_w_in = dt("mg_w_in", [D, 2368])
    mla_w_uq = dt("mla_w_uq", [512, 1536])
    mla_w_ukv = dt("mla_w_ukv", [256, 2048])
    mg_w_o = dt("mg_w_o", [D, D])
    diff_w_in = dt("diff_w_in", [D, 6144])
    diff_w_o = dt("diff_w_o", [D, D])
    g0_in = dt("g0", [128, 1024])
    ropeM_in = dt("ropeM", [SEQ, 2, 256])
    ropeG_in = dt("ropeG", [SEQ, 2, 1024])
    lqk_in = dt("lqk", [128, 4, 128])
    gsub_in = dt("gsub", [128, 256])
    out = nc.dram_tensor("out", [2, SEQ, D], F32, kind="ExternalOutput").ap()
    QT_all = nc.dram_tensor("QT_all", [2, 18, 128, 24, 128], BF16, kind="Internal").ap()
    KT_all = nc.dram_tensor("KT_all", [2, 128, 17, 18 * 128], BF16, kind="Internal").ap()
    V_all = nc.dram_tensor("V_all", [2, 18, 128, 2048], BF16, kind="Internal").ap()
    OT_all = nc.dram_tensor("OT_all", [NBLK, 128, KC, TB], BF16, kind="Internal").ap()
    W1c = nc.dram_tensor("W1c", [4, 22, 128, KC, 256], BF16, kind="Internal").ap()
    W3c = nc.dram_tensor("W3c", [4, 22, 128, KC, 256], BF16, kind="Internal").ap()
    W2c = nc.dram_tensor("W2c", [4, KC, 128, FC, 128], BF16, kind="Internal").ap()
    XT = nc.dram_tensor("XT", [NBLK, 128, KC, TB], F32, kind="Internal").ap()
    if dbg:
        dbg_xt = nc.dram_tensor("dbg_xt", [NBLK, 128, KC, TB], F32, kind="ExternalOutput").ap()

    kb = KB(nc)
    with kb.es:
        top = kb.es
        identF = kb.sb(top, "identF", [128, 128], F32)
        onesF = kb.sb(top, "onesF", [128, 128], F32)
        modT = [kb.sb(top, "modT%d" % l, [128, NMOD, 3], F32) for l in range(2)]
        mod1p = [kb.sb(top, "mod1p%d" % l, [128, NMOD, 3], F32) for l in range(2)]
        modh = [kb.sb(top, "modh%d" % l, [128, NMOD, 3], F32) for l in range(2)]
        lng = kb.sb(top, "lng", [128, 6, KC], F32)
        lnb = kb.sb(top, "lnb", [128, 6, KC], F32)
        sT = kb.sb(top, "sT", [128, KC, 3], BF16)
        identB = kb.sb(top, "identB", [128, 128], BF16)

        kb.op("pool", lambda e: e.memset(onesF[:], 1.0), w=["onesF"])
        kb.op("pool", lambda e: e.memset(identF[:], 0.0), w=["identF"])
        kb.op("pool", lambda e: e.affine_select(out=identF[:], in_=onesF[:], pattern=[[-1, 128]],
                                                compare_op=ALU.is_equal, fill=0.0, base=0,
                                                channel_multiplier=1), r=["onesF"], w=["identF"])
        kb.op("dve", lambda e: e.tensor_copy(out=identB[:], in_=identF[:]), r=["identF"], w=["identB"])
        kb.dma("sp", lng[:], ln_gT, w=["lng"])
        kb.dma("sp", lnb[:], ln_bT, w=["lnb"])

        def conv_jobs(f):
            l_, fi_ = f // 2, f % 2
            w1v = ffn_w1[l_, fi_].rearrange("(kc p) f -> p kc f", p=128)
            w3v = ffn_w3[l_, fi_].rearrange("(kc p) f -> p kc f", p=128)
            w2v = ffn_w2[l_, fi_].rearrange("(fc p) n -> p fc n", p=128)
            jobs = []
            for fb in range(22):
                jobs.append((W1c[f, fb], w1v[:, :, fb * 256:(fb + 1) * 256], ("W1c", f, fb)))
                jobs.append((W3c[f, fb], w3v[:, :, fb * 256:(fb + 1) * 256], ("W3c", f, fb)))
            for n in range(KC):
                jobs.append((W2c[f, n], w2v[:, :, n * 128:(n + 1) * 128], ("W2c", f, n)))
            return jobs

        conv_pending = []

        def conv_emit(n):
            for _ in range(min(n, len(conv_pending))):
                dst, src, key = conv_pending.pop(0)
                kb.dma("pool", dst, src, w=[key], sem="conv")

        def blk_rows(blk):
            return 2 if blk == 0 else (0 if blk <= 4 else 1)

        def tile_src(blk, i):
            if blk == 0:
                return ctx_in[i // 2, (i % 2) * 128:(i % 2) * 128 + 128, :]
            b = 0 if blk <= 4 else 1
            t0 = ((blk - 1) % 4) * TB + i * 128
            return x_in[b, t0:t0 + 128, :]

        def stage_in():
            with kb.scope() as st:
                xtok = [kb.sb(st, "xtok", [128, D], F32) for _ in range(2)]
                xbs = [kb.sb(st, "xbs", [128, KC, TB], F32) for _ in range(2)]
                pst = [kb.ps(st, "pst", [128, 512], F32) for _ in range(4)]
                n = 0
                for blk in range(NBLK):
                    xb = xbs[blk % 2]
                    for i in range(4):
                        t = xtok[n % 2]
                        kb.dma("sp", t[:], tile_src(blk, i), w=[("xtok", n % 2)], sem=("ld_xtok", n % 2))
                        for q in range(4):
                            pb = pst[(n * 4 + q) % 4]
                            for kk in range(4):
                                kc = q * 4 + kk
                                kb.op("pe", lambda e, pb=pb, kk=kk, kc=kc, t=t: e.transpose(
                                    out=pb[:, kk * 128:(kk + 1) * 128], in_=t[:, kc * 128:(kc + 1) * 128],
                                    identity=identF[:]), r=[("xtok", n % 2), "identF"],
                                    w=[("pst", (n * 4 + q) % 4)], inc=(kk == 3))
                            eng = "dve" if q % 2 == 0 else "act"
                            dst = xb[:, q * 4:(q + 1) * 4, i * 128:(i + 1) * 128]
                            src = pb[:].rearrange("p (a b) -> p a b", a=4)
                            if eng == "dve":
                                kb.op("dve", lambda e, dst=dst, src=src: e.tensor_copy(out=dst, in_=src),
                                      r=[("pst", (n * 4 + q) % 4)], w=[("xb", blk % 2, i, q)])
                            else:
                                kb.op("act", lambda e, dst=dst, src=src: e.activation(out=dst, in_=src, func=AF.Copy),
                                      r=[("pst", (n * 4 + q) % 4)], w=[("xb", blk % 2, i, q)])
                        n += 1
                    kb.dma("sp", XT[blk], xb[:], r=[("xb", blk % 2, i, q) for i in range(4) for q in range(4)],
                           w=[("XT", blk)], sem=("st_xb", blk % 2))

        def stage_out():
            with kb.scope() as st:
                xtok = [kb.sb(st, "xtok", [128, D], F32) for _ in range(2)]
                xbs = [kb.sb(st, "xbs", [128, KC, TB], F32) for _ in range(2)]
                pst = [kb.ps(st, "pst", [128, 512], F32) for _ in range(4)]
                n = 0
                for blk in range(1, NBLK):
                    xb = xbs[blk % 2]
                    kb.dma("sp", xb[:], XT[blk], r=[("XT", blk)], w=[("xb", blk % 2)], sem=("ld_xb", blk % 2))
                    b = 0 if blk <= 4 else 1
                    for i in range(4):
                        t = xtok[n % 2]
                        for q in range(4):
                            pb = pst[(n * 4 + q) % 4]
                            for kk in range(4):
                                kc = q * 4 + kk
                                kb.op("pe", lambda e, pb=pb, kk=kk, kc=kc, xb=xb, i=i: e.transpose(
                                    out=pb[:, kk * 128:(kk + 1) * 128], in_=xb[:, kc, i * 128:(i + 1) * 128],
                                    identity=identF[:]), r=[("xb", blk % 2), "identF"],
                                    w=[("pst", (n * 4 + q) % 4)], inc=(kk == 3))
                            dst = t[:, q * 512:(q + 1) * 512]
                            if q % 2 == 0:
                                kb.op("dve", lambda e, dst=dst, pb=pb: e.tensor_copy(out=dst, in_=pb[:]),
                                      r=[("pst", (n * 4 + q) % 4)], w=[("xtok", n % 2, q)])
                            else:
                                kb.op("act", lambda e, dst=dst, pb=pb: e.activation(out=dst, in_=pb[:], func=AF.Copy),
                                      r=[("pst", (n * 4 + q) % 4)], w=[("xtok", n % 2, q)])
                        t0 = ((blk - 1) % 4) * TB + i * 128
                        kb.dma("sp", out[b, t0:t0 + 128, :], t[:], r=[("xtok", n % 2, q) for q in range(4)],
                               w=[("out", blk, i)], sem=("st_xtok", n % 2))
                        n += 1

        def stage_silu():
            with kb.scope() as st:
                cf = kb.sb(st, "cf", [128, KC, 3], F32)
                kb.dma("sp", cf[:], cT_in, w=["cf"])
                kb.op("act", lambda e: e.activation(out=sT[:], in_=cf[:], func=AF.Silu), r=["cf"], w=["sT"])

        def stage_mod(l):
            with kb.scope() as st:
                slots = [kb.sb(st, "wa", [128, KC, 512], BF16) for _ in range(3)]
                ring = Ring(kb, "rw", slots)
                wv = w_ada[l].rearrange("(kc p) n -> p kc n", p=128)
                ring.units = [wv[:, :, c * 512:(c + 1) * 512] for c in range(36)]
                pm = kb.ps(st, "pm", [128, 512], F32)
                bt = kb.sb(st, "bt", [128, NMOD * 3], F32)
                kb.dma("sp", bt[:], b_adaT[l], w=["bt"])
                ring.issue_upto(3)
                for c in range(36):
                    wa = ring.buf(c)
                    for sub in range(4):
                        ch = c * 4 + sub
                        for kc in range(KC):
                            kb.op("pe", lambda e, wa=wa, sub=sub, kc=kc, ch=ch: e.matmul(
                                pm[:, ch * 3:ch * 3 + 3], lhsT=wa[:, kc, sub * 128:(sub + 1) * 128],
                                rhs=sT[:, kc, :], start=(kc == 0), stop=(kc == KC - 1)),
                                r=[ring.key(c), "sT"], w=["pm"], inc=(kc == KC - 1 and sub == 3))
                    ring.issue_upto(c + 4)
                m2 = modT[l][:].rearrange("p a b -> p (a b)")
                kb.op("dve", lambda e: e.tensor_tensor(out=m2, in0=pm[:, 0:NMOD * 3], in1=bt[:], op=ALU.add),
                      r=["pm", "bt"], w=[("modT", l)])
                kb.op("dve", lambda e: e.tensor_scalar_add(out=mod1p[l][:].rearrange("p a b -> p (a b)"), in0=m2,
                                                           scalar1=1.0), r=[("modT", l)], w=[("mod1p", l)])
                kb.op("dve", lambda e: e.tensor_scalar_mul(out=modh[l][:].rearrange("p a b -> p (a b)"), in0=m2,
                                                           scalar1=0.5), r=[("modT", l)], w=[("modh", l)])

        def mod_thunks(l, st, pm):
            slots = [kb.sb(st, "wa", [128, KC, 512], BF16) for _ in range(3)]
            ring = Ring(kb, "rw", slots)
            wv = w_ada[l].rearrange("(kc p) n -> p kc n", p=128)
            ring.units = [wv[:, :, c * 512:(c + 1) * 512] for c in range(36)]
            bt = kb.sb(st, "bt", [128, NMOD * 3], F32)
            th = []

            def first():
                kb.dma("sp", bt[:], b_adaT[l], w=["bt"])
                ring.issue_upto(3)
            th.append(first)

            def chunk(c):
                wa = ring.buf(c)
                for sub in range(4):
                    ch = c * 4 + sub
                    for kc in range(KC):
                        kb.op("pe", lambda e, sub=sub, kc=kc, ch=ch: e.matmul(
                            pm[:, ch * 3:ch * 3 + 3], lhsT=wa[:, kc, sub * 128:(sub + 1) * 128],
                            rhs=sT[:, kc, :], start=(kc == 0), stop=(kc == KC - 1)),
                            r=[ring.key(c), "sT"], w=["pm"], inc=(kc == KC - 1 and sub == 3))
                ring.issue_upto(c + 4)
            for c in range(36):
                th.append(lambda c=c: chunk(c))

            def last():
                m2 = modT[l][:].rearrange("p a b -> p (a b)")
                kb.op("dve", lambda e: e.tensor_tensor(out=m2, in0=pm[:, 0:NMOD * 3], in1=bt[:], op=ALU.add),
                      r=["pm", "bt"], w=[("modT", l)])
                kb.op("dve", lambda e: e.tensor_scalar_add(out=mod1p[l][:].rearrange("p a b -> p (a b)"), in0=m2,
                                                           scalar1=1.0), r=[("modT", l)], w=[("mod1p", l)])
                kb.op("dve", lambda e: e.tensor_scalar_mul(out=modh[l][:].rearrange("p a b -> p (a b)"), in0=m2,
                                                           scalar1=0.5), r=[("modT", l)], w=[("modh", l)])
            th.append(last)
            return th

        def mod_ap(t, l, j, s, kc, row):
            idx = (j * 3 + s) * KC + kc
            return t[l][:, idx, row:row + 1]

        def emit_modulate(l, j, blk, xs, hT, hkey):
            row = blk_rows(blk)
            npc = xs.shape[1]
            for half in range(KC // npc):
                kb.dma("sp", xs[:], XT[blk][:, half * npc:(half + 1) * npc, :], r=[("XT", blk)], w=["xs"], sem="ld_xs")
                for k8 in range(npc):
                    kc = half * npc + k8
                    kb.op("dve", lambda e, k8=k8, kc=kc: e.tensor_scalar(
                        out=hT[:, kc, :], in0=xs[:, k8, :], scalar1=mod_ap(mod1p, l, j, 1, kc, row),
                        scalar2=mod_ap(modT, l, j, 0, kc, row), op0=ALU.mult, op1=ALU.add),
                        r=["xs", ("mod1p", l), ("modT", l)], w=[(hkey, kc)])

        def emit_outproj_epilogue(l, j, blk, nfc, uT, ukey, w2ring, w2base, xz, gate_t, lnidx, py, pst,
                                  tmp, after_load=None, xk="xz", defer=False, prev_tail=None):
            row = blk_rows(blk)
            yg, sq, s1, s2, mean, msq, var, rstd = tmp
            kb.dma("sp", xz[:], XT[blk], r=[("XT", blk)], w=[(xk, kc) for kc in range(KC)], sem=("ld_xz", xk))
            if after_load is not None:
                after_load()
            for ncn in range(KC):
                u = w2base + ncn
                wb = w2ring.buf(u)
                pb = py[ncn % 2]
                for fc in range(nfc):
                    kb.op("pe", lambda e, wb=wb, fc=fc, pb=pb: e.matmul(
                        pb[:], lhsT=wb[:, fc, :], rhs=uT[:, fc, :], start=(fc == 0), stop=(fc == nfc - 1)),
                        r=[w2ring.key(u), (ukey, fc)], w=[("py", ncn % 2)], inc=(fc == nfc - 1))
                w2ring.issue_upto(u + 1 + len(w2ring.slots))
                if prev_tail:
                    for _ in range(5):
                        if prev_tail:
                            prev_tail.pop(0)()
                ygb = yg[ncn % 2]
                kb.op("act", lambda e, ygb=ygb, pb=pb, ncn=ncn: e.activation(
                    out=ygb[:], in_=pb[:], func=AF.Copy, scale=mod_ap(gate_t, l, j, 2, ncn, row)),
                    r=[("py", ncn % 2), ("gate", l)], w=[("yg", ncn % 2)])
                kb.op("dve", lambda e, ygb=ygb, ncn=ncn: e.scalar_tensor_tensor(
                    out=xz[:, ncn, :], in0=xz[:, ncn, :], scalar=ALPHA, in1=ygb[:], op0=ALU.mult, op1=ALU.add),
                    r=[("yg", ncn % 2)], w=[(xk, ncn)])
                kb.op("act", lambda e, ncn=ncn: e.activation(out=sq[ncn % 2][:], in_=xz[:, ncn, :], func=AF.Square),
                      r=[(xk, ncn)], w=[("sq", ncn % 2)])
                if ncn == 0:
                    kb.op("dve", lambda e: e.tensor_copy(out=s1[:], in_=xz[:, 0, :]), r=[(xk, 0)], w=["s1"])
                    kb.op("dve", lambda e: e.tensor_copy(out=s2[:], in_=sq[0][:]), r=[("sq", 0)], w=["s2"])
                else:
                    kb.op("dve", lambda e, ncn=ncn: e.tensor_add(out=s1[:], in0=s1[:], in1=xz[:, ncn, :]),
                          r=[(xk, ncn)], w=["s1"])
                    kb.op("dve", lambda e, ncn=ncn: e.tensor_add(out=s2[:], in0=s2[:], in1=sq[ncn % 2][:]),
                          r=[("sq", ncn % 2)], w=["s2"])
            tail = []
            T = tail.append
            T(lambda: kb.op("pe", lambda e: e.matmul(pst[0][:], lhsT=onesF[:], rhs=s1[:], start=True, stop=True),
                            r=["s1", "onesF"], w=["pst0"]))
            T(lambda: kb.op("pe", lambda e: e.matmul(pst[1][:], lhsT=onesF[:], rhs=s2[:], start=True, stop=True),
                            r=["s2", "onesF"], w=["pst1"]))
            T(lambda: kb.op("dve", lambda e: e.tensor_scalar_mul(out=mean[:], in0=pst[0][:], scalar1=1.0 / D),
                            r=["pst0"], w=["mean"]))
            T(lambda: kb.op("dve", lambda e: e.tensor_tensor(out=msq[:], in0=mean[:], in1=mean[:], op=ALU.mult),
                            r=["mean"], w=["msq"]))
            T(lambda: kb.op("dve", lambda e: e.scalar_tensor_tensor(out=var[:], in0=pst[1][:], scalar=1.0 / D,
                                                                    in1=msq[:], op0=ALU.mult, op1=ALU.subtract),
                            r=["pst1", "msq"], w=["var"]))
            T(lambda: kb.op("dve", lambda e: e.tensor_scalar_add(out=var[:], in0=var[:], scalar1=EPS), r=["var"],
                            w=["var"]))
            T(lambda: kb.op("act", lambda e: e.activation(out=var[:], in_=var[:], func=AF.Sqrt), r=["var"], w=["var"]))
            T(lambda: kb.op("dve", lambda e: e.reciprocal(out=rstd[:], in_=var[:]), r=["var"], w=["rstd"]))
            for kc in range(KC):
                T(lambda kc=kc: kb.op("dve", lambda e: e.tensor_tensor(out=xz[:, kc, :], in0=xz[:, kc, :], in1=mean[:],
                                                                       op=ALU.subtract), r=["mean"], w=[(xk, kc)]))
                T(lambda kc=kc: kb.op("dve", lambda e: e.tensor_tensor(out=xz[:, kc, :], in0=xz[:, kc, :], in1=rstd[:],
                                                                       op=ALU.mult), r=["rstd"], w=[(xk, kc)]))
                T(lambda kc=kc: kb.op("act", lambda e: e.activation(
                    out=xz[:, kc, :], in_=xz[:, kc, :], func=AF.Identity, scale=lng[:, lnidx, kc:kc + 1],
                    bias=lnb[:, lnidx, kc:kc + 1]), r=["lng", "lnb"], w=[(xk, kc)]))
            T(lambda: kb.dma("sp", XT[blk], xz[:], r=[(xk, kc) for kc in range(KC)], w=[("XT", blk)], sem=("st_xz", xk)))
            if defer:
                return tail
            for t_ in tail:
                t_()
            return []

        def alloc_epi_tmp(st):
            yg = [kb.sb(st, "yg", [128, TB], F32) for _ in range(2)]
            sq = [kb.sb(st, "sq", [128, TB], F32) for _ in range(2)]
            rest = [kb.sb(st, nm, [128, TB], F32) for nm in ("s1", "s2", "mean", "msq", "var", "rstd")]
            return (yg, sq) + tuple(rest)

        def stage_ffn(l, j, fi, blocks):
            with kb.scope() as st:
                xz = kb.sb(st, "xz", [128, KC, TB], F32)
                xs = kb.sb(st, "xs", [128, 8, TB], F32)
                hT = kb.sb(st, "hT", [128, KC, TB], BF16)
                uT = kb.sb(st, "uT", [128, FC, TB], BF16)
                sil = [kb.sb(st, "sil", [128, TB], BF16) for _ in range(2)]
                tmp = alloc_epi_tmp(st)
                w1s = [kb.sb(st, "w1s", [128, KC, 256], BF16) for _ in range(2)]
                w3s = [kb.sb(st, "w3s", [128, KC, 256], BF16) for _ in range(2)]
                w2s = [kb.sb(st, "w2s", [128, FC, 128], BF16) for _ in range(3)]
                pu1 = [kb.ps(st, "pu1", [128, TB], F32) for _ in range(2)]
                pu3 = [kb.ps(st, "pu3", [128, TB], F32) for _ in range(2)]
                py = [kb.ps(st, "py", [128, TB], F32) for _ in range(2)]
                pst = [kb.ps(st, "pst", [128, TB], F32) for _ in range(2)]
                r1 = Ring(kb, "r1", w1s)
                r3 = Ring(kb, "r3", w3s)
                r2 = Ring(kb, "r2", w2s)
                f = l * 2 + fi
                ng = len(blocks)
                for g in range(ng):
                    if f == 0 and g == 0:
                        w1v = ffn_w1[l, fi].rearrange("(kc p) f -> p kc f", p=128)
                        w3v = ffn_w3[l, fi].rearrange("(kc p) f -> p kc f", p=128)
                        w2v = ffn_w2[l, fi].rearrange("(fc p) n -> p fc n", p=128)
                        r1.units += [w1v[:, :, fb * 256:(fb + 1) * 256] for fb in range(22)]
                        r3.units += [w3v[:, :, fb * 256:(fb + 1) * 256] for fb in range(22)]
                        r2.units += [w2v[:, :, n * 128:(n + 1) * 128] for n in range(KC)]
                        continue
                    r1.units += [W1c[f, fb] for fb in range(22)]
                    r3.units += [W3c[f, fb] for fb in range(22)]
                    r2.units += [W2c[f, n] for n in range(KC)]
                if f == 0:
                    wc = lambda: kb.wait_all("pool", "conv")
                    r1.pre[22] = wc
                    r3.pre[22] = wc
                    r2.pre[KC] = wc
                conv_emit(1000)
                if f != 0:
                    kb.wait_all("pool", "conv")
                r1.issue_upto(2)
                r3.issue_upto(2)
                r2.issue_upto(3)
                emit_modulate(l, j, blocks[0], xs, hT, "hT")
                tail = []
                for g, blk in enumerate(blocks):
                    for fc in range(FC):
                        if fc >= 1:
                            for _ in range(3):
                                if tail:
                                    tail.pop(0)()
                        u = g * 22 + fc // 2
                        fl = fc % 2
                        for (ring, pbank, nm) in ((r1, pu1, "pu1"), (r3, pu3, "pu3")):
                            wb = ring.buf(u)
                            pb = pbank[fc % 2]
                            for kc in range(KC):
                                kb.op("pe", lambda e, wb=wb, pb=pb, kc=kc, fl=fl: e.matmul(
                                    pb[:], lhsT=wb[:, kc, fl * 128:(fl + 1) * 128], rhs=hT[:, kc, :],
                                    start=(kc == 0), stop=(kc == KC - 1)),
                                    r=[ring.key(u), ("hT", kc)], w=[(nm, fc % 2)], inc=(kc == KC - 1))
                        if fl == 1:
                            r1.issue_upto(u + 3)
                            r3.issue_upto(u + 3)
                        sb_ = sil[fc % 2]
                        kb.op("act", lambda e, sb_=sb_, fc=fc: e.activation(out=sb_[:], in_=pu1[fc % 2][:], func=AF.Silu),
                              r=[("pu1", fc % 2)], w=[("sil", fc % 2)])
                        kb.op("dve", lambda e, sb_=sb_, fc=fc: e.tensor_tensor(
                            out=uT[:, fc, :], in0=pu3[fc % 2][:], in1=sb_[:], op=ALU.mult),
                            r=[("pu3", fc % 2), ("sil", fc % 2)], w=[("uT", fc)])
                    nxt = None
                    if g + 1 < ng:
                        nxt = lambda g=g: emit_modulate(l, j, blocks[g + 1], xs, hT, "hT")
                    while tail:
                        tail.pop(0)()
                    tail = emit_outproj_epilogue(l, j, blk, FC, uT, "uT", r2, g * KC, xz, modh, l * 3 + j, py, pst,
                                                 tmp, after_load=nxt, defer=True)
                while tail:
                    tail.pop(0)()

        NT = 18

        def tile_of(blk, i):
            if blk == 0:
                return i // 2, i % 2, False, 0
            b = 0 if blk <= 4 else 1
            t = ((blk - 1) % 4) * 4 + i
            return b, 2 + t, True, t

        def ot_dst(b, t):
            if t < 2:
                return 0, (b * 2 + t) * 128
            return 1 + 4 * b + (t - 2) // 4, ((t - 2) % 4) * 128

        def evac(i, dst, src, r, w):
            if i % 2 == 0:
                kb.op("dve", lambda e: e.tensor_copy(out=dst, in_=src), r=r, w=w)
            else:
                kb.op("act", lambda e: e.activation(out=dst, in_=src, func=AF.Copy), r=r, w=w)

        class TrCtx:
            def __init__(self, ptr, eng=None):
                self.ptr = ptr
                self.n = 0
                self.eng = eng

            def run(self, srcs, dst_t, base, rkeys, wkey):
                for j0 in range(0, len(srcs), 8):
                    grp = srcs[j0:j0 + 8]
                    n_ = len(grp)
                    bi = self.n % len(self.ptr)
                    self.n += 1
                    pb = self.ptr[bi]
                    w_ = grp[0].shape[-1]
                    for jj, s in enumerate(grp):
                        kb.op("pe", lambda e, s=s, jj=jj, pb=pb: e.transpose(
                            out=pb[0:w_, jj * 128:(jj + 1) * 128], in_=s, identity=identB[:]),
                            r=list(rkeys) + ["identB"], w=[("ptr", bi)], inc=(jj == n_ - 1))
                    src = pb[0:w_, 0:n_ * 128].rearrange("p (a b) -> p a b", a=n_)
                    dst = dst_t[0:w_, base + j0:base + j0 + n_, :]
                    evac(1 if self.eng == "act" else self.n, dst, src, [("ptr", bi)], [wkey])

        def rms_scale(ss, n, tmpv, extra=1.0):
            kb.op("dve", lambda e: e.tensor_scalar(out=ss, in0=ss, scalar1=1.0 / n, scalar2=EPS, op0=ALU.mult,
                                                   op1=ALU.add), r=[tmpv], w=[tmpv])
            kb.op("act", lambda e: e.activation(out=ss, in_=ss, func=AF.Sqrt), r=[tmpv], w=[tmpv])
            kb.op("dve", lambda e: e.reciprocal(out=ss, in_=ss), r=[tmpv], w=[tmpv])
            if extra != 1.0:
                kb.op("dve", lambda e: e.tensor_scalar_mul(out=ss, in0=ss, scalar1=extra), r=[tmpv], w=[tmpv])

        def rope_cast(dst, src, cos, sin, lat, tmps, rk, wk, tk=""):
            if not lat:
                kb.op("dve", lambda e: e.tensor_copy(out=dst, in_=src), r=rk, w=[wk])
                return
            h = src.shape[-1] // 2
            H = src.shape[1]
            t1, t2 = [t[:, 0:H * h].rearrange("p (a b) -> p a b", a=H) for t in tmps]
            x1, x2 = src[:, :, 0:h], src[:, :, h:2 * h]
            kb.op("dve", lambda e: e.tensor_tensor(out=t1, in0=x1, in1=cos, op=ALU.mult), r=rk, w=["rt1" + tk])
            kb.op("dve", lambda e: e.tensor_tensor(out=t2, in0=x2, in1=sin, op=ALU.mult), r=rk, w=["rt2" + tk])
            kb.op("dve", lambda e: e.tensor_tensor(out=dst[:, :, 0:h], in0=t1, in1=t2, op=ALU.subtract),
                  r=["rt1" + tk, "rt2" + tk], w=[wk])
            kb.op("dve", lambda e: e.tensor_tensor(out=t1, in0=x2, in1=cos, op=ALU.mult), r=rk, w=["rt1" + tk])
            kb.op("dve", lambda e: e.tensor_tensor(out=t2, in0=x1, in1=sin, op=ALU.mult), r=rk, w=["rt2" + tk])
            kb.op("dve", lambda e: e.tensor_tensor(out=dst[:, :, h:2 * h], in0=t1, in1=t2, op=ALU.add),
                  r=["rt1" + tk, "rt2" + tk], w=[wk])

        def stage_attn_a(l):
            with kb.scope() as st:
                xs = kb.sb(st, "xs", [128, 4 if l == 0 else 8, TB], F32)
                hT = kb.sb(st, "hT", [128, KC, TB], BF16)
                NWS = 2 if l == 0 else 3
                wslots = [kb.sb(st, "wi", [128, KC, 512], BF16) for _ in range(NWS)]
                ring = Ring(kb, "rw", wslots)
                if l == 0:
                    zsets = [[kb.sb(st, "zsec", [128, zw_], F32) for _ in range(4)] for zw_ in (832, 1024, 512)]
                    wbfB = kb.sb(st, "wbfB", [128, 1280], BF16)
                    sqjB = kb.sb(st, "sqjB", [128, 1024], F32)
                    ssvB = kb.sb(st, "ssvB", [128, 16], F32)
                    rtB = [kb.sb(st, "rtB", [128, 512], F32) for _ in range(2)]
                    trsB = kb.sb(st, "trsB", [128, 8, 128], BF16)
                else:
                    zsets = [[kb.sb(st, "zsec", [128, 2048], F32) for _ in range(4)] for _ in range(2)]
                wbf = kb.sb(st, "wbf", [128, 2048], BF16)
                rt = [kb.sb(st, "rt", [128, 512 if l == 0 else 1024], F32) for _ in range(2)]
                gw = 512 if l == 0 else 1024
                ropeGs = [kb.sb(st, "ropeG", [128, 2, gw], F32) for _ in range(2)]
                ropeMs = [kb.sb(st, "ropeM", [128, 2, 256], F32) for _ in range(2)]
                trs = kb.sb(st, "trs", [128, 16, 128], BF16)
                trs2 = kb.sb(st, "trs2", [128, 9, 128], BF16)
                ssv = kb.sb(st, "ssv", [128, 16], F32)
                mhalf = kb.sb(st, "mhalf", [128, 16], F32)
                sqj = kb.sb(st, "sqj", [128, 1024], F32)
                pz = [kb.ps(st, "pz", [128, 512], F32) for _ in range(3)]
                ptr = [kb.ps(st, "ptr", [128, 1024], BF16) for _ in range(2)]
                tr = TrCtx(ptr, eng="act")
                kb.op("pool", lambda e: e.memset(mhalf[:], -0.5), w=["mhalf"])
                if l == 0:
                    wuq = kb.sb(st, "wuq", [128, 4, 1536], BF16)
                    wukv = kb.sb(st, "wukv", [128, 2, 2048], BF16)
                    gt = kb.sb(st, "gt", [128, 1024], F32)
                    c6 = kb.sb(st, "c6", [128, 6, 128], BF16)
                    qm = kb.sb(st, "qm", [128, 1536], F32)
                    kvm = kb.sb(st, "kvm", [128, 2048], F32)
                    kb.dma("pool", wuq[:], mla_w_uq.rearrange("(kc p) n -> p kc n", p=128), w=["wuq"])
                    kb.dma("pool", wukv[:], mla_w_ukv.rearrange("(kc p) n -> p kc n", p=128), w=["wukv"])
                    kb.dma("sp", gt[:], g0_in, w=["gt"])
                    wv = mg_w_in.rearrange("(kc p) n -> p kc n", p=128)
                    secs = [[(0, 512), (512, 320)], [(832, 512), (1344, 512)], [(1856, 512)]]
                else:
                    wv = diff_w_in.rearrange("(kc p) n -> p kc n", p=128)
                    secs = [[(s * 2048 + c * 512, 512) for c in range(4)] for s in range(3)]
                blocks = list(range(NBLK))
                cn = dict(pz=0)

                def secs_of(blk):
                    if l == 1 and blk == 0:
                        return [1, 2]
                    return [0, 1, 2]

                def act_evac(dst, src, r, w):
                    kb.op("act", lambda e: e.activation(out=dst, in_=src, func=AF.Copy), r=r, w=w)

                def rms_pow(ss, n, key):
                    k = ss.shape[-1]
                    kb.op("dve", lambda e: e.tensor_scalar(out=ss, in0=ss, scalar1=1.0 / n, scalar2=EPS, op0=ALU.mult,
                                                           op1=ALU.add), r=[key], w=[key])
                    kb.op("pool", lambda e: e.tensor_tensor(out=ss, in0=ss, in1=mhalf[:, 0:k], op=ALU.pow),
                          r=[key, "mhalf"], w=[key])

                def make_post(blk, s, i, zs, zp):
                    b, t, lat, pt = tile_of(blk, i)
                    z = zs[i]
                    zk = [("zsec", zp, i)]
                    qdst = QT_all[b, t]
                    kdst = KT_all[b]
                    kcols = slice(t * 128, (t + 1) * 128)
                    th = []
                    T = th.append
                    ropeG = ropeGs[i % 2]
                    ropeM = ropeMs[i % 2]
                    rgk = ("ropeG", i % 2)
                    rmk = ("ropeM", i % 2)

                    def rope_load(ii):
                        b_, t_, lat_, pt_ = tile_of(blk, ii)
                        if not lat_ or (l == 1 and s == 2):
                            return
                        if l == 0 and s == 0:
                            kb.dma("sp", ropeMs[ii % 2][:], ropeM_in[pt_ * 128:(pt_ + 1) * 128], w=[("ropeM", ii % 2)],
                                   sem=("ld_ropeM", ii % 2))
                        else:
                            kb.dma("sp", ropeGs[ii % 2][:], ropeG_in[pt_ * 128:(pt_ + 1) * 128, :, 0:gw],
                                   w=[("ropeG", ii % 2)], sem=("ld_ropeG", ii % 2))
                    if i == 0:
                        rope_load(0)
                    if i < 3:
                        T(lambda: rope_load(i + 1))
                    if l == 1:
                        if s == 2:
                            T(lambda: kb.op("act", lambda e: e.activation(out=wbf[:], in_=z[:], func=AF.Copy), r=zk, w=["wbf"]))
                            T(lambda: kb.dma("sp", V_all[b, t], wbf[:], r=["wbf"], w=[("V", b, t)], sem="st_wbf"))
                            return th
                        w3 = wbf[:].rearrange("p (a b) -> p a b", a=16)
                        T(lambda: rope_cast(w3, z[:].rearrange("p (a b) -> p a b", a=16),
                                            ropeG[:, 0, :].rearrange("p (a b) -> p a b", a=16),
                                            ropeG[:, 1, :].rearrange("p (a b) -> p a b", a=16), lat, rt,
                                            zk + [rgk], "wbf"))
                        T(lambda: tr.run([wbf[:, h * 128:(h + 1) * 128] for h in range(8)], trs, 0, ["wbf"], "trs"))
                        T(lambda: tr.run([wbf[:, h * 128:(h + 1) * 128] for h in range(8, 16)], trs, 8, ["wbf"], "trs"))
                        if s == 0:
                            T(lambda: kb.dma("sp", qdst[:, 0:16, :], trs[:], r=["trs"], w=[("Q", b, t)], sem="st_trs"))
                        else:
                            T(lambda: kb.dma("sp", kdst[:, 0:16, kcols], trs[:], r=["trs"], w=[("K", b, t)],
                                             sem="st_trs"))
                        return th
                    if s == 0:
                        def n1():
                            kb.op("act", lambda e: e.activation(out=sqj[:, 0:512], in_=z[:, 0:512], func=AF.Square,
                                                                accum_out=ssv[:, 0:1]), r=zk, w=["sqj", "ssv"])
                            kb.op("act", lambda e: e.activation(out=sqj[:, 512:768], in_=z[:, 512:768], func=AF.Square,
                                                                accum_out=ssv[:, 1:2]), r=zk, w=["sqj", "ssv"])
                            kb.op("dve", lambda e: e.tensor_scalar_mul(out=ssv[:, 1:2], in0=ssv[:, 1:2], scalar1=2.0),
                                  r=["ssv"], w=["ssv"])
                            rms_pow(ssv[:, 0:2], 512.0, "ssv")
                        T(n1)

                        def n2():
                            kb.op("dve", lambda e: e.scalar_tensor_tensor(
                                out=wbf[:, 0:512], in0=z[:, 0:512], scalar=ssv[:, 0:1], in1=gt[:, 0:512],
                                op0=ALU.mult, op1=ALU.mult), r=zk + ["ssv", "gt"], w=["wbf"])
                            kb.op("dve", lambda e: e.scalar_tensor_tensor(
                                out=wbf[:, 512:768], in0=z[:, 512:768], scalar=ssv[:, 1:2], in1=gt[:, 512:768],
                                op0=ALU.mult, op1=ALU.mult), r=zk + ["ssv", "gt"], w=["wbf"])
                        T(n2)
                        T(lambda: tr.run([wbf[:, h * 128:(h + 1) * 128] for h in range(6)], c6, 0, ["wbf"], "c6"))

                        def mmq(c):
                            pb = pz[cn["pz"] % 3]
                            pk = ("pz", cn["pz"] % 3)
                            cn["pz"] += 1
                            for kc in range(4):
                                kb.op("pe", lambda e, kc=kc: e.matmul(
                                    pb[:], lhsT=c6[:, kc, :], rhs=wuq[:, kc, c * 512:(c + 1) * 512],
                                    start=(kc == 0), stop=(kc == 3)), r=["c6", "wuq"], w=[pk], inc=(kc == 3))
                            act_evac(qm[:, c * 512:(c + 1) * 512], pb[:], [pk], ["qm"])

                        def mmkv(c):
                            pb = pz[cn["pz"] % 3]
                            pk = ("pz", cn["pz"] % 3)
                            cn["pz"] += 1
                            for kc in range(2):
                                kb.op("pe", lambda e, kc=kc: e.matmul(
                                    pb[:], lhsT=c6[:, 4 + kc, :], rhs=wukv[:, kc, c * 512:(c + 1) * 512],
                                    start=(kc == 0), stop=(kc == 1)), r=["c6", "wukv"], w=[pk], inc=(kc == 1))
                            act_evac(kvm[:, c * 512:(c + 1) * 512], pb[:], [pk], ["kvm"])
                        for c in range(3):
                            T(lambda c=c: mmq(c))
                        for c in range(4):
                            T(lambda c=c: mmkv(c))
                        qm3 = qm[:].rearrange("p (a b) -> p a b", a=8)
                        kv3 = kvm[:].rearrange("p (a b) -> p a b", a=8)
                        T(lambda: kb.op("act", lambda e: e.activation(
                            out=wbf[:, 0:1024].rearrange("p (a b) -> p a b", a=8), in_=qm3[:, :, 0:128], func=AF.Copy),
                            r=["qm"], w=["wbf"]))
                        T(lambda: rope_cast(wbf[:, 1024:1536].rearrange("p (a b) -> p a b", a=8), qm3[:, :, 128:192],
                                            ropeM[:, 0, :].rearrange("p (a b) -> p a b", a=8),
                                            ropeM[:, 1, :].rearrange("p (a b) -> p a b", a=8), lat, rt,
                                            ["qm", rmk], "wbf"))
                        T(lambda: rope_cast(wbf[:, 1536:1600].rearrange("p (a b) -> p a b", a=1),
                                            z[:, 768:832].rearrange("p (a b) -> p a b", a=1),
                                            ropeM[:, 0, 0:32].rearrange("p (a b) -> p a b", a=1),
                                            ropeM[:, 1, 0:32].rearrange("p (a b) -> p a b", a=1), lat, rt,
                                            zk + [rmk], "wbf"))
                        T(lambda: tr.run([wbf[:, h * 128:(h + 1) * 128] for h in range(8)], trs, 0, ["wbf"], "trs"))
                        T(lambda: tr.run([wbf[:, 1024 + h * 64:1024 + (h + 1) * 64] for h in range(8)], trs, 8,
                                         ["wbf"], "trs"))

                        def st1():
                            kb.dma("sp", qdst[:, 0:8, :], trs[:, 0:8, :], r=["trs"], w=[("Q", b, t)], sem="st_trs")
                            kb.dma("sp", qdst[0:64, 16:24, :], trs[0:64, 8:16, :], r=["trs"], w=[("Q", b, t)],
                                   sem="st_trs")
                        T(st1)
                        T(lambda: tr.run([wbf[:, 1536:1600]], trs2, 8, ["wbf"], "trs2"))
                        T(lambda: kb.op("act", lambda e: e.activation(
                            out=wbf[:, 0:1024].rearrange("p (a b) -> p a b", a=8), in_=kv3[:, :, 0:128], func=AF.Copy),
                            r=["kvm"], w=["wbf"]))
                        T(lambda: kb.op("dve", lambda e: e.tensor_copy(
                            out=wbf[:, 1024:2048].rearrange("p (a b) -> p a b", a=8), in_=kv3[:, :, 128:256]),
                            r=["kvm"], w=["wbf"]))
                        T(lambda: tr.run([wbf[:, h * 128:(h + 1) * 128] for h in range(8)], trs2, 0, ["wbf"], "trs2"))

                        def st2():
                            kb.dma("sp", kdst[:, 0:8, kcols], trs2[:, 0:8, :], r=["trs2"], w=[("K", b, t)], sem="st_trs2")
                            kb.dma("sp", kdst[0:64, 10, kcols], trs2[0:64, 8, :], r=["trs2"], w=[("K", b, t)],
                                   sem="st_trs2")
                            kb.dma("sp", V_all[b, t][:, 0:1024], wbf[:, 1024:2048], r=["wbf"], w=[("V", b, t)],
                                   sem="st_wbf")
                        T(st2)
                        return th
                    nh = 8 if s == 1 else 2
                    gsl = gt[:, 768:896] if s == 1 else gt[:, 896:1024]
                    z3 = z[:, 0:nh * 128].rearrange("p (a b) -> p a b", a=nh)

                    def g1():
                        kb.op("act", lambda e: e.activation(out=sqjB[:, 0:nh * 128], in_=z[:, 0:nh * 128],
                                                            func=AF.Square), r=zk, w=["sqjB"])
                        kb.op("dve", lambda e: e.tensor_reduce(
                            out=ssvB[:, 0:nh], in_=sqjB[:, 0:nh * 128].rearrange("p (a b) -> p a b", a=nh),
                            axis=AX.X, op=ALU.add), r=["sqjB"], w=["ssvB"])
                        rms_pow(ssvB[:, 0:nh], 128.0, "ssvB")
                    T(g1)

                    def g2():
                        for h in range(nh):
                            kb.op("dve", lambda e, h=h: e.scalar_tensor_tensor(
                                out=z[:, h * 128:(h + 1) * 128], in0=z[:, h * 128:(h + 1) * 128],
                                scalar=ssvB[:, h:h + 1], in1=gsl, op0=ALU.mult, op1=ALU.mult),
                                r=["ssvB", "gt"], w=zk)
                    T(g2)
                    T(lambda: rope_cast(wbfB[:, 0:nh * 128].rearrange("p (a b) -> p a b", a=nh), z3,
                                        ropeG[:, 0, 0:nh * 64].rearrange("p (a b) -> p a b", a=nh),
                                        ropeG[:, 1, 0:nh * 64].rearrange("p (a b) -> p a b", a=nh), lat, rtB,
                                        zk + [rgk], "wbfB", tk="B"))
                    T(lambda: tr.run([wbfB[:, h * 128:(h + 1) * 128] for h in range(nh)], trsB, 0, ["wbfB"], "trsB"))
                    if s == 1:
                        T(lambda: kb.dma("sp", qdst[:, 8:16, :], trsB[:, 0:8, :], r=["trsB"], w=[("Q", b, t)], sem="st_trsB"))
                    else:
                        def st3():
                            kb.dma("sp", kdst[:, 8:10, kcols], trsB[:, 0:2, :], r=["trsB"], w=[("K", b, t)], sem="st_trsB")
                            kb.op("act", lambda e: e.activation(out=wbfB[:, 1024:1280], in_=z[:, 256:512], func=AF.Copy),
                                  r=zk, w=["wbfB"])
                            kb.dma("sp", V_all[b, t][:, 1024:1280], wbfB[:, 1024:1280], r=["wbfB"], w=[("V", b, t)],
                                   sem="st_wbfB")
                        T(st3)
                    return th

                for blk in blocks:
                    for s in secs_of(blk):
                        ring.units += [wv[:, :, c0:c0 + w_] for (c0, w_) in secs[s]]
                ring.issue_upto(NWS)
                uidx = 0
                if l == 0:
                    for f_ in (1, 2, 3):
                        conv_pending.extend(conv_jobs(f_))
                pending = []
                pendB = []
                zp = 0

                def pump(nA, nB):
                    for _ in range(nA):
                        if pending:
                            pending.pop(0)()
                    for _ in range(nB):
                        if pendB:
                            pendB.pop(0)[1]()

                for blk in blocks:
                    emit_modulate(l, 1, blk, xs, hT, "hT")
                    for s in secs_of(blk):
                        if l == 0:
                            zp = s
                            if s == 0:
                                while pending:
                                    pending.pop(0)()
                            else:
                                while any(tag == s for tag, _ in pendB):
                                    pendB.pop(0)[1]()
                        zs = zsets[zp]
                        off = 0
                        nslots = len(secs[s]) * 4
                        per = (len(pending) + nslots - 1) // nslots if pending else 0
                        for (c0, w_) in secs[s]:
                            wb = ring.buf(uidx)
                            for i in range(4):
                                pb = pz[cn["pz"] % 3]
                                pk = ("pz", cn["pz"] % 3)
                                cn["pz"] += 1
                                for kc in range(KC):
                                    kb.op("pe", lambda e, pb=pb, wb=wb, kc=kc, i=i, w_=w_: e.matmul(
                                        pb[:, 0:w_], lhsT=hT[:, kc, i * 128:(i + 1) * 128], rhs=wb[:, kc, 0:w_],
                                        start=(kc == 0), stop=(kc == KC - 1)),
                                        r=[ring.key(uidx), ("hT", kc)], w=[pk], inc=(kc == KC - 1))
                                act_evac(zs[i][:, off:off + w_], pb[:, 0:w_], [pk], [("zsec", zp, i)])
                                conv_emit(1)
                                if l == 0:
                                    pump(4, 3)
                                else:
                                    pump(per, 0)
                            ring.issue_upto(uidx + 1 + NWS)
                            uidx += 1
                            off += w_
                        if l == 1:
                            while pending:
                                pending.pop(0)()
                        for i in range(4):
                            th_ = make_post(blk, s, i, zs, zp)
                            if l == 0 and s != 0:
                                pendB += [(s, t_) for t_ in th_]
                            else:
                                pending += th_
                        if l == 1:
                            zp ^= 1
                while pending or pendB:
                    pump(1, 1)

        def stage_attn_b(l, host_mod=None):
            SHIFT = 10.0
            if l == 0:
                groups = [dict(ks=list(range(8)) + [10], vc0=0, nvs=8, dv=128, kc0=0, scale=192.0 ** -0.5, mla=True,
                               q0=0),
                          dict(ks=[8, 9], vc0=1024, nvs=2, dv=128, kc0=8, scale=128.0 ** -0.5, mla=False, q0=8)]
            else:
                groups = [dict(ks=list(range(8 * g, 8 * g + 8)), vc0=1024 * g, nvs=4, dv=256, kc0=8 * g,
                               scale=128.0 ** -0.5, mla=False, q0=8 * g) for g in range(2)]
            diff = (l == 1)
            with kb.scope() as st:
                KTs = kb.sb(st, "KTs", [128, 9, NT * 128], BF16)
                Vaf = kb.sb(st, "Vaf", [128, NT * 1032], BF16)
                QTs = [kb.sb(st, "QTs", [128, 16, 128], BF16) for _ in range(3)]
                NPT = 8
                PTr = [kb.sb(st, "PTr", [128, 4, 128], BF16) for _ in range(NPT)]
                Obfs = [kb.sb(st, "Obf", [128, 1024], BF16) for _ in range(2)]
                OTs = kb.sb(st, "OTs", [128, 8, 128], BF16)
                sv = kb.sb(st, "sv", [128, 8], F32)
                negC = kb.sb(st, "negC", [128, 1], F32)
                O0n = kb.sb(st, "O0n", [128, 256], F32)
                O32 = kb.sb(st, "O32", [128, 256], F32)
                sqj = kb.sb(st, "sqj", [128, 256], F32)
                NSR = 5 if host_mod is None else 4
                Sr = [kb.ps(st, "Sr", [128, 512], F32) for _ in range(NSR)]
                hosted = []
                if host_mod is not None:
                    pm_ = kb.ps(st, "pm", [128, 512], F32)
                    hosted = mod_thunks(host_mod, st, pm_)
                Op = [kb.ps(st, "Op", [128, 512], F32) for _ in range(2)]
                ptr = [kb.ps(st, "ptr", [128, 1024], BF16)]
                tr = TrCtx(ptr)
                kb.op("pool", lambda e: e.memset(negC[:], -SHIFT), w=["negC"])
                mhalfB = kb.sb(st, "mhalfB", [128, 1], F32)
                kb.op("pool", lambda e: e.memset(mhalfB[:], -0.5), w=["mhalfB"])
                if l == 0:
                    kb.op("pool", lambda e: e.memset(KTs[64:128, 8, :], 0.0), w=[("KTs", 8)])
                    for qi in range(3):
                        kb.op("pool", lambda e, qi=qi: e.memset(QTs[qi][64:128, 8:16, :], 0.0), w=[("QTs", qi)])
                if diff:
                    lamv = kb.sb(st, "lamv", [128, 4], F32)
                    lqk = kb.sb(st, "lqk", [128, 4, 128], F32)
                    gsub = kb.sb(st, "gsub", [128, 256], F32)
                    kb.dma("sp", lqk[:], lqk_in, w=["lqk"])
                    kb.dma("sp", gsub[:], gsub_in, w=["gsub"])
                    kb.op("dve", lambda e: e.tensor_tensor(out=lqk[:, 0, :], in0=lqk[:, 0, :], in1=lqk[:, 1, :],
                                                           op=ALU.mult), r=["lqk"], w=["lqk"])
                    kb.op("dve", lambda e: e.tensor_tensor(out=lqk[:, 2, :], in0=lqk[:, 2, :], in1=lqk[:, 3, :],
                                                           op=ALU.mult), r=["lqk"], w=["lqk"])
                    kb.op("dve", lambda e: e.reduce_sum(out=lamv[:, 0:1], in_=lqk[:, 0, :], axis=AX.X), r=["lqk"],
                          w=["lamv"])
                    kb.op("dve", lambda e: e.reduce_sum(out=lamv[:, 1:2], in_=lqk[:, 2, :], axis=AX.X), r=["lqk"],
                          w=["lamv"])
                    kb.op("act", lambda e: e.activation(out=lamv[:, 0:2], in_=lamv[:, 0:2], func=AF.Exp),
                          r=["lamv"], w=["lamv"])
                    kb.op("dve", lambda e: e.tensor_tensor(out=lamv[:, 2:3], in0=lamv[:, 1:2], in1=lamv[:, 0:1],
                                                           op=ALU.subtract), r=["lamv"], w=["lamv"])
                    kb.op("dve", lambda e: e.tensor_scalar_add(out=lamv[:, 2:3], in0=lamv[:, 2:3],
                                                               scalar1=-LAMBDA_INIT1), r=["lamv"], w=["lamv"])
                cnt = dict(s=0, p=0, q=0, u=0, o=0)
                for b in range(2):
                    for gi, g in enumerate(groups):
                        dv, nvs = g["dv"], g["nvs"]
                        dvp = dv + 1
                        Va = Vaf[:, 0:NT * nvs * dvp].rearrange("p (t h d) -> p t h d", t=NT, h=nvs)
                        for si, ks in enumerate(g["ks"]):
                            npart = 64 if (g["mla"] and ks == 10) else 128
                            kb.dma("sp", KTs[0:npart, si, :], KT_all[b][0:npart, ks, :], w=[("KTs", si)], sem="ld_kv")
                        kb.op("pool", lambda e, Va=Va, dv=dv: e.memset(Va[:, :, :, dv:dv + 1], 1.0), w=["Va"])
                        for hs in range(nvs):
                            c0 = g["vc0"] + hs * dv
                            kb.dma("sp", Va[:, :, hs, 0:dv], V_all[b][:, :, c0:c0 + dv].rearrange("t p d -> p t d"),
                                   w=["Va"], sem="ld_kv")
                        qtiles = list(range(NT)) if l == 0 else list(range(2, NT))
                        nq = len(g["ks"]) if False else 8

                        def load_q(t, g=g, b=b):
                            qi_ = cnt["q"] % 3
                            QT = QTs[qi_]
                            qkey = ("QTs", qi_)
                            qsem = ("ld_qt", qi_)
                            cnt["q"] += 1
                            if g["mla"]:
                                kb.dma("sp", QT[:, 0:8, :], QT_all[b, t][:, 0:8, :], w=[qkey], sem=qsem)
                                kb.dma("sp", QT[0:64, 8:16, :], QT_all[b, t][0:64, 16:24, :], w=[qkey], sem=qsem)
                            else:
                                kb.dma("sp", QT[:, 0:8, :], QT_all[b, t][:, g["q0"]:g["q0"] + 8, :], w=[qkey], sem=qsem)
                            return QT, qkey

                        def chunks_of(t):
                            nkt = 2 if t < 2 else NT
                            return [(kt0, min(4, nkt - kt0)) for kt0 in range(0, nkt, 4)]

                        qinfo = {}
                        qinfo[qtiles[0]] = load_q(qtiles[0])
                        units = [(t, h) for t in qtiles for h in range(8)]
                        pend = {}

                        def emit_S(ui, ci, g=g):
                            t, h = units[ui]
                            QT, qkey = qinfo[t]
                            kt0, nk = chunks_of(t)[ci]
                            bi = cnt["s"] % NSR
                            cnt["s"] += 1
                            bank = Sr[bi]
                            kslot = (h // 4) if (l == 0 and not g["mla"]) else h
                            for j in range(nk):
                                kt = kt0 + j
                                parts = [(KTs[:, kslot, kt * 128:(kt + 1) * 128], QT[:, h, :])]
                                if g["mla"]:
                                    parts.append((KTs[:, 8, kt * 128:(kt + 1) * 128], QT[:, 8 + h, :]))
                                for pi, (ka, qa) in enumerate(parts):
                                    kb.op("pe", lambda e, ka=ka, qa=qa, j=j, pi=pi, np_=len(parts), bank=bank: e.matmul(
                                        bank[:, j * 128:(j + 1) * 128], lhsT=ka, rhs=qa, start=(pi == 0),
                                        stop=(pi == np_ - 1)),
                                        r=[qkey, ("KTs", kslot)] + ([("KTs", 8)] if g["mla"] else []),
                                        w=[("Sr", bi)], inc=(j == nk - 1 and pi == len(parts) - 1))
                            pi_ = cnt["p"] % NPT
                            cnt["p"] += 1
                            ptb = PTr[pi_]
                            kb.op("act", lambda e, ptb=ptb, bank=bank, nk=nk: e.activation(
                                out=ptb[:, 0:nk, :], in_=bank[:, 0:nk * 128].rearrange("p (a b) -> p a b", a=nk),
                                func=AF.Exp, scale=g["scale"], bias=negC[:, 0:1]),
                                r=[("Sr", bi), "negC"], w=[("PTr", pi_)])
                            pend[(ui, ci)] = (ptb, ("PTr", pi_))

                        def emit_PV(ui, ci, g=g, Va=Va, dvp=dvp):
                            t, h = units[ui]
                            ch = chunks_of(t)
                            kt0, nk = ch[ci]
                            ptb, pkey = pend.pop((ui, ci))
                            ob = ui % 2
                            if diff:
                                vs = h // 2
                            elif g["mla"]:
                                vs = h
                            else:
                                vs = h // 4
                            for j in range(nk):
                                kt = kt0 + j
                                first = (ci == 0 and j == 0)
                                last = (ci == len(ch) - 1 and j == nk - 1)
                                kb.op("pe", lambda e, ptb=ptb, j=j, kt=kt, first=first, last=last, ob=ob, vs=vs: e.matmul(
                                    Op[ob][:, 0:dvp], lhsT=ptb[:, j, :], rhs=Va[:, kt, vs, :], start=first, stop=last),
                                    r=[pkey, "Va"], w=[("Op", ob)], inc=(j == nk - 1))

                        def finalize(ui, g=g, dv=dv, b=b):
                            t, h = units[ui]
                            ob = ui % 2
                            Obf = Obfs[(ui // 8) % 2]
                            okey = ("Obf", (ui // 8) % 2)
                            O = Op[ob]
                            if not diff:
                                kb.op("dve", lambda e: e.reciprocal(out=sv[:, 0:1], in_=O[:, dv:dv + 1]),
                                      r=[("Op", ob)], w=["sv0"])
                                kb.op("dve", lambda e: e.tensor_scalar(
                                    out=Obf[:, h * 128:(h + 1) * 128], in0=O[:, 0:128], scalar1=sv[:, 0:1],
                                    scalar2=None, op0=ALU.mult), r=[("Op", ob), "sv0"], w=[okey])
                            elif h % 2 == 0:
                                kb.op("dve", lambda e: e.reciprocal(out=sv[:, 0:1], in_=O[:, dv:dv + 1]),
                                      r=[("Op", ob)], w=["sv0"])
                                kb.op("dve", lambda e: e.tensor_scalar(
                                    out=O0n[:], in0=O[:, 0:256], scalar1=sv[:, 0:1], scalar2=None, op0=ALU.mult),
                                    r=[("Op", ob), "sv0"], w=["O0n"])
                            else:
                                hh = h // 2
                                kb.op("dve", lambda e: e.reciprocal(out=sv[:, 1:2], in_=O[:, dv:dv + 1]),
                                      r=[("Op", ob)], w=["sv1"])
                                kb.op("dve", lambda e: e.tensor_tensor(out=sv[:, 1:2], in0=sv[:, 1:2], in1=lamv[:, 2:3],
                                                                       op=ALU.mult), r=["sv1", "lamv"], w=["sv1"])
                                kb.op("dve", lambda e: e.scalar_tensor_tensor(
                                    out=O32[:], in0=O[:, 0:256], scalar=sv[:, 1:2], in1=O0n[:], op0=ALU.mult,
                                    op1=ALU.add), r=[("Op", ob), "sv1", "O0n"], w=["O32"])
                                kb.op("dve", lambda e: e.tensor_tensor(out=sqj[:], in0=O32[:], in1=O32[:], op=ALU.mult),
                                      r=["O32"], w=["sqj"])
                                kb.op("dve", lambda e: e.reduce_sum(out=sv[:, 2:3], in_=sqj[:], axis=AX.X),
                                      r=["sqj"], w=["sv2"])
                                kb.op("dve", lambda e: e.tensor_scalar(out=sv[:, 2:3], in0=sv[:, 2:3], scalar1=1.0 / 256.0,
                                                                       scalar2=EPS, op0=ALU.mult, op1=ALU.add),
                                      r=["sv2"], w=["sv2"])
                                kb.op("pool", lambda e: e.tensor_tensor(out=sv[:, 2:3], in0=sv[:, 2:3], in1=mhalfB[:, 0:1],
                                                                        op=ALU.pow), r=["sv2", "mhalfB"], w=["sv2"])
                                kb.op("dve", lambda e: e.tensor_scalar_mul(out=sv[:, 2:3], in0=sv[:, 2:3],
                                                                           scalar1=1.0 - LAMBDA_INIT1),
                                      r=["sv2"], w=["sv2"])
                                kb.op("dve", lambda e, hh=hh: e.scalar_tensor_tensor(
                                    out=Obf[:, hh * 256:(hh + 1) * 256], in0=O32[:], scalar=sv[:, 2:3], in1=gsub[:],
                                    op0=ALU.mult, op1=ALU.mult), r=["O32", "sv2", "gsub"], w=[okey])
                            if h == 7:
                                if hosted:
                                    hosted.pop(0)()
                                tr.run([Obf[:, c * 128:(c + 1) * 128] for c in range(8)], OTs, 0, [okey], "OTs")
                                oblk, ocol = ot_dst(b, t)
                                kb.dma("sp", OT_all[oblk][:, g["kc0"]:g["kc0"] + 8, ocol:ocol + 128], OTs[:],
                                       r=["OTs"], w=[("OT", oblk)], sem="st_ots")

                        nu = len(units)
                        for ci in range(len(chunks_of(units[0][0]))):
                            emit_S(0, ci)
                        for ui in range(nu):
                            t, h = units[ui]
                            if h == 0:
                                ti = qtiles.index(t)
                                if ti + 1 < len(qtiles):
                                    qinfo[qtiles[ti + 1]] = load_q(qtiles[ti + 1])
                            nci = len(chunks_of(t))
                            ncn = len(chunks_of(units[ui + 1][0])) if ui + 1 < nu else 0
                            for ci in range(max(nci, ncn)):
                                if ci < ncn:
                                    emit_S(ui + 1, ci)
                                if ci < nci:
                                    emit_PV(ui, ci)
                            finalize(ui)
                while hosted:
                    hosted.pop(0)()

        def stage_attn_c(l, blocks):
            with kb.scope() as st:
                xzs = [kb.sb(st, "xz", [128, KC, TB], F32) for _ in range(2)]
                uTs = [kb.sb(st, "oT", [128, KC, TB], BF16) for _ in range(2)]
                tmp = alloc_epi_tmp(st)
                wos = [kb.sb(st, "wos", [128, KC, 128], BF16) for _ in range(KC)]
                py = [kb.ps(st, "py", [128, TB], F32) for _ in range(2)]
                pst = [kb.ps(st, "pst", [128, TB], F32) for _ in range(2)]
                ro = Ring(kb, "wo16", wos)
                wov = (mg_w_o if l == 0 else diff_w_o).rearrange("(fc p) n -> p fc n", p=128)
                ro.units += [wov[:, :, n * 128:(n + 1) * 128] for n in range(KC)]
                ro.issue_upto(KC)
                tail = []
                for g, blk in enumerate(blocks):
                    uT = uTs[g % 2]
                    ukey = "oT%d" % (g % 2)
                    kb.dma("sp", uT[:], OT_all[blk], r=[("OT", blk)], w=[(ukey, fc) for fc in range(KC)], sem=("ld_ot", g % 2))
                    tail = emit_outproj_epilogue(l, 1, blk, KC, uT, ukey, ro, g * KC, xzs[g % 2], modT, l * 3 + 1,
                                                 py, pst, tmp, xk="xz%d" % (g % 2), defer=True, prev_tail=tail)
                while tail:
                    tail.pop(0)()

        conv_pending.extend(conv_jobs(0))
        conv_emit(1000)
        stage_in()
        stage_silu()
        todo = stages if stages is not None else ["mod0", "ffn00", "attn0", "ffn02", "mod1", "ffn10", "attn1", "ffn12"]
        allb = list(range(NBLK))
        for s in todo:
            if s == "mod0":
                stage_mod(0)
            elif s == "mod1":
                if "attn0" not in todo:
                    stage_mod(1)
            elif s == "ffn00":
                stage_ffn(0, 0, 0, allb)
            elif s == "ffn02":
                stage_ffn(0, 2, 1, allb)
            elif s == "ffn10":
                stage_ffn(1, 0, 0, allb)
            elif s == "ffn12":
                stage_ffn(1, 2, 1, allb[1:])
            elif s == "attn0":
                stage_attn_a(0)
                stage_attn_b(0, host_mod=(1 if "mod1" in todo else None))
                stage_attn_c(0, allb)
            elif s == "attn1":
                stage_attn_a(1)
                stage_attn_b(1)
                stage_attn_c(1, allb[1:])
            elif s.startswith("ffnp"):
                stage_ffn(0, 0, 0, [int(c) for c in s[4:]])
        if dbg:
            with kb.scope() as st:
                for blk in range(NBLK):
                    kb.dma("sp", dbg_xt[blk], XT[blk], r=[("XT", blk)], w=[("dbg", blk)])
        stage_out()
        kb.barrier(skip=())
    return nc


_ROPE = None


def rope_tables():
    global _ROPE
    if _ROPE is None:
        pos = np.arange(SEQ)
        r = (pos // 64).astype(np.float32)[:, None]
        col = (pos % 64).astype(np.float32)[:, None]

        def tab(rot_dim, heads):
            nf = rot_dim // 4
            inv = (np.float32(10000.0) ** (-np.arange(nf, dtype=np.float32) / np.float32(nf))).astype(np.float32)
            ang = np.concatenate([r * inv, col * inv], -1).astype(np.float32)
            cs = np.stack([np.tile(np.cos(ang), (1, heads)), np.tile(np.sin(ang), (1, heads))], 1)
            return np.ascontiguousarray(cs, dtype=np.float32)
        _ROPE = (tab(64, 8), tab(128, 16))
    return _ROPE


def host_inputs(inputs, core):
    b0 = 2 * core
    f = lambda a: np.ascontiguousarray(a, dtype=np.float32)
    c3 = np.stack([inputs["c"][b0], inputs["c"][b0 + 1], inputs["c_ctx"]], 0)
    cT = c3.reshape(3, KC, 128).transpose(2, 1, 0)
    b_ada = inputs["b_ada"].reshape(2, NMOD, 128).transpose(0, 2, 1)
    b_adaT = np.repeat(b_ada[:, :, :, None], 3, axis=3).reshape(2, 128, NMOD * 3)
    lg = inputs["ln_g"].reshape(6, KC, 128).transpose(2, 0, 1)
    lb = inputs["ln_b"].reshape(6, KC, 128).transpose(2, 0, 1)
    rep = lambda v: np.broadcast_to(np.asarray(v, np.float32).reshape(1, -1), (128, np.asarray(v).size))
    g0 = np.concatenate([rep(inputs["mla_g_cq"][0]), rep(inputs["mla_g_ckv"][0]), rep(inputs["gqa_g_q"][0]),
                         rep(inputs["gqa_g_k"][0])], axis=1)
    lqk = np.stack([rep(inputs["diff_lq1"][0]), rep(inputs["diff_lk1"][0]), rep(inputs["diff_lq2"][0]),
                    rep(inputs["diff_lk2"][0])], axis=1)
    ropeM, ropeG = rope_tables()
    m = {
        "mg_w_in": f(inputs["mg_w_in"][0]), "mla_w_uq": f(inputs["mla_w_uq"][0]), "mla_w_ukv": f(inputs["mla_w_ukv"][0]),
        "mg_w_o": f(inputs["mg_w_o"][0]), "diff_w_in": f(inputs["diff_w_in"][0]), "diff_w_o": f(inputs["diff_w_o"][0]),
        "g0": f(g0), "ropeM": ropeM, "ropeG": ropeG, "lqk": f(lqk), "gsub": f(rep(inputs["diff_g_sub"][0])),
        "x": f(inputs["x"][b0:b0 + 2]), "ctx": f(inputs["ctx"][b0:b0 + 2]), "cT": f(cT),
        "w_ada": f(inputs["w_ada"]), "b_adaT": f(b_adaT), "ln_gT": f(lg), "ln_bT": f(lb),
        "ffn_w1": f(inputs["ffn_w1"]), "ffn_w3": f(inputs["ffn_w3"]), "ffn_w2": f(inputs["ffn_w2"]),
    }
    return m


def kernel(**inputs):
    nc = build_program()
    in_maps = [host_inputs(inputs, c) for c in range(NCORES)]
    res = run_bass_kernel_spmd(nc, in_maps, core_ids=list(range(NCORES)))
    return np.concatenate([np.asarray(r["out"]) for r in res.results], axis=0).astype(np.float32)
```
